# Optimizing a Trainium2 kernel written in Bass

```python
import math
import jax, jax.numpy as jnp
from jax import lax
import numpy as np

D_MODEL = 1024
BATCH = 8
SEQ = 4096
DEPTH = 1

HEAD_DIM = 64
N_HEADS_A = 8
N_KV_A = 2
GROUP_A = N_HEADS_A // N_KV_A
WINDOW_A = 128
N_HEADS_B = 8
DILATED_PATTERNS = ((128, 1), (512, 4), (2048, 16))
WIDTH_A = N_HEADS_A * HEAD_DIM
WIDTH_B = N_HEADS_B * HEAD_DIM
D_MIX = WIDTH_A + WIDTH_B
KV_WIDTH_A = N_KV_A * HEAD_DIM
D_IN_PROJ = WIDTH_A + 2 * KV_WIDTH_A + 3 * WIDTH_B
N_BIAS_HEADS = N_HEADS_A + N_HEADS_B
NUM_BUCKETS = 32
MAX_DISTANCE = 1024
D_FF = 4 * D_MODEL
PLE_DIM = 256
EPS = 1e-6
NEG = -1e30

kernel_name = "hybrid_wingqa_dilated_sandwich_layer"


def rmsnorm(x, g):
    xf = x.astype(jnp.float32)
    y = xf * lax.rsqrt(jnp.mean(xf * xf, axis=-1, keepdims=True) + EPS)
    return (y * g.astype(jnp.float32)).astype(x.dtype)


def t5_bucket(rel):
    half = NUM_BUCKETS // 2
    max_exact = half // 2
    sign = jnp.where(rel > 0, half, 0)
    n = jnp.abs(rel)
    nf = jnp.maximum(n, 1).astype(jnp.float32)
    large = max_exact + (jnp.log(nf / max_exact) / math.log(MAX_DISTANCE / max_exact)
                         * (half - max_exact)).astype(jnp.int32)
    large = jnp.minimum(large, half - 1)
    return sign + jnp.where(n < max_exact, n, large)


def band_rel(block):
    qi = jnp.arange(block)[:, None]
    ki = jnp.arange(3 * block)[None, :]
    return ki - block - qi


def banded_attention(q, k, v, bias, half_window, block, sink):
    b_, hk, g, L, dh = q.shape
    nb = -(-L // block)
    lp = nb * block
    q = jnp.pad(q, ((0, 0), (0, 0), (0, 0), (0, lp - L), (0, 0)))
    kv_pad = ((0, 0), (0, 0), (block, lp - L + block), (0, 0))
    kp = jnp.pad(k, kv_pad).reshape(b_, hk, nb + 2, block, dh)
    vp = jnp.pad(v, kv_pad).reshape(b_, hk, nb + 2, block, dh)
    kw = jnp.concatenate([kp[:, :, :-2], kp[:, :, 1:-1], kp[:, :, 2:]], axis=3)
    vw = jnp.concatenate([vp[:, :, :-2], vp[:, :, 1:-1], vp[:, :, 2:]], axis=3)
    qb = q.reshape(b_, hk, g, nb, block, dh)
    s = jnp.einsum('bhgnqd,bhnkd->bhgnqk', qb, kw).astype(jnp.float32) * (dh ** -0.5)
    s = s + bias[:, :, None]
    rel = band_rel(block)
    kpos = jnp.arange(nb)[:, None, None] * block + jnp.arange(3 * block)[None, None, :] - block
    valid = (jnp.abs(rel) <= half_window)[None] & (kpos >= 0) & (kpos < L)
    s = jnp.where(valid, s, NEG)
    m = jnp.max(s, axis=-1, keepdims=True)
    if sink is not None:
        sinkb = sink.astype(jnp.float32)[None, :, :, None, None, None]
        m = jnp.maximum(m, sinkb)
    e = jnp.exp(s - m)
    denom = jnp.sum(e, axis=-1, keepdims=True)
    if sink is not None:
        denom = denom + jnp.exp(sinkb - m)
    o = jnp.einsum('bhgnqk,bhnkd->bhgnqd', e, vw.astype(jnp.float32)) / denom
    lse = (jnp.log(denom) + m)[..., 0]
    o = o.reshape(b_, hk, g, lp, dh)[:, :, :, :L].astype(k.dtype)
    lse = lse.reshape(b_, hk, g, lp)[..., :L]
    return o, lse


def windowed_gqa_sink(qa, ka, va, bias_table, sink):
    b_, s_, _ = qa.shape
    q = qa.reshape(b_, s_, N_KV_A, GROUP_A, HEAD_DIM).transpose(0, 2, 3, 1, 4)
    k = ka.reshape(b_, s_, N_KV_A, HEAD_DIM).transpose(0, 2, 1, 3)
    v = va.reshape(b_, s_, N_KV_A, HEAD_DIM).transpose(0, 2, 1, 3)
    bias = bias_table[t5_bucket(band_rel(WINDOW_A))][..., :N_HEADS_A]
    bias = bias.transpose(2, 0, 1).reshape(N_KV_A, GROUP_A, WINDOW_A, 3 * WINDOW_A)
    o, _ = banded_attention(q, k, v, bias.astype(jnp.float32), WINDOW_A, WINDOW_A,
                            sink.reshape(N_KV_A, GROUP_A))
    return o.transpose(0, 3, 1, 2, 4).reshape(b_, s_, WIDTH_A)


def dilated_mixture(qb_, kb_, vb_, bias_table):
    b_, s_, _ = qb_.shape
    to_heads = lambda t: t.reshape(b_, s_, N_HEADS_B, HEAD_DIM).transpose(0, 2, 1, 3)
    q, k, v = to_heads(qb_), to_heads(kb_), to_heads(vb_)
    outs, lses = [], []
    for window, dil in DILATED_PATTERNS:
        half = window // (2 * dil)
        ls = s_ // dil
        sub = lambda t: t.reshape(b_, N_HEADS_B, ls, dil, HEAD_DIM).transpose(0, 1, 3, 2, 4) \
                         .reshape(b_, N_HEADS_B * dil, ls, HEAD_DIM)
        bias = bias_table[t5_bucket(band_rel(half) * dil)][..., N_HEADS_A:]
        bias = jnp.repeat(bias.transpose(2, 0, 1), dil, axis=0)[:, None]
        o, lse = banded_attention(sub(q)[:, :, None], sub(k), sub(v),
                                  bias.astype(jnp.float32), half, half, None)
        o = o.reshape(b_, N_HEADS_B, dil, ls, HEAD_DIM).transpose(0, 1, 3, 2, 4) \
             .reshape(b_, N_HEADS_B, s_, HEAD_DIM)
        lse = lse.reshape(b_, N_HEADS_B, dil, ls).transpose(0, 1, 3, 2).reshape(b_, N_HEADS_B, s_)
        outs.append(o)
        lses.append(lse)
    w = jax.nn.softmax(jnp.stack(lses, axis=0), axis=0)
    o = jnp.sum(w[..., None] * jnp.stack(outs, axis=0).astype(jnp.float32), axis=0).astype(q.dtype)
    return o.transpose(0, 2, 1, 3).reshape(b_, s_, WIDTH_B)


def setup_inputs(seed: int = 0) -> dict:
    key = jax.random.key(seed)
    ks = jax.random.split(key, 20)
    nrm = lambda k, shape, scale: (jax.random.normal(k, shape, jnp.float32) * scale)
    gain = lambda k, n: 1.0 + nrm(k, (DEPTH, n), 0.02)
    return {
        "x": nrm(ks[0], (BATCH, SEQ, D_MODEL), 1.0),
        "p": nrm(ks[1], (DEPTH, BATCH, SEQ, PLE_DIM), 1.0),
        "rel_bias_table": nrm(ks[2], (NUM_BUCKETS, N_BIAS_HEADS), 0.5),
        "g_pre_mix": gain(ks[3], D_MODEL),
        "w_in": nrm(ks[4], (DEPTH, D_MODEL, D_IN_PROJ), D_MODEL ** -0.5),
        "sink_a": nrm(ks[5], (DEPTH, N_HEADS_A), 0.5),
        "g_out_a": gain(ks[6], WIDTH_A),
        "g_out_b": gain(ks[7], WIDTH_B),
        "w_o": nrm(ks[8], (DEPTH, D_MIX, D_MODEL), D_MIX ** -0.5),
        "g_post_mix": gain(ks[9], D_MODEL),
        "g_pre_mlp": gain(ks[10], D_MODEL),
        "w_up": nrm(ks[11], (DEPTH, D_MODEL, D_FF), D_MODEL ** -0.5),
        "w_down": nrm(ks[12], (DEPTH, D_FF, D_MODEL), D_FF ** -0.5),
        "g_post_mlp": gain(ks[13], D_MODEL),
        "w_ple_proj": nrm(ks[14], (DEPTH, PLE_DIM, D_MODEL), PLE_DIM ** -0.5),
        "w_ple_gate": nrm(ks[15], (DEPTH, D_MODEL, D_MODEL), D_MODEL ** -0.5),
        "b_ple_gate": nrm(ks[16], (DEPTH, D_MODEL), 0.02),
        "g_post_ple": gain(ks[17], D_MODEL),
    }


def reference(x, p, rel_bias_table, g_pre_mix, w_in, sink_a, g_out_a, g_out_b, w_o,
              g_post_mix, g_pre_mlp, w_up, w_down, g_post_mlp, w_ple_proj, w_ple_gate,
              b_ple_gate, g_post_ple):
    h = x
    offs = np.cumsum([0, WIDTH_A, KV_WIDTH_A, KV_WIDTH_A, WIDTH_B, WIDTH_B, WIDTH_B])
    for i in range(DEPTH):
        u = rmsnorm(h, g_pre_mix[i])
        proj = jnp.einsum('bsd,de->bse', u, w_in[i])
        qa, ka, va, qb, kb, vb = [proj[..., offs[j]:offs[j + 1]] for j in range(6)]
        o_a = rmsnorm(windowed_gqa_sink(qa, ka, va, rel_bias_table, sink_a[i]), g_out_a[i])
        o_b = rmsnorm(dilated_mixture(qb, kb, vb, rel_bias_table), g_out_b[i])
        mix = jnp.einsum('bse,ed->bsd', jnp.concatenate([o_a, o_b], axis=-1), w_o[i])
        h = h + rmsnorm(mix, g_post_mix[i])
        v_ = rmsnorm(h, g_pre_mlp[i])
        a = jax.nn.relu(jnp.einsum('bsd,df->bsf', v_, w_up[i]))
        ff = jnp.einsum('bsf,fd->bsd', a * a, w_down[i])
        h = h + rmsnorm(ff, g_post_mlp[i])
        gate = jax.nn.sigmoid(jnp.einsum('bsd,de->bse', h, w_ple_gate[i]) + b_ple_gate[i])
        ple = jnp.einsum('bsk,kd->bsd', p[i], w_ple_proj[i])
        h = h + rmsnorm(gate * ple, g_post_ple[i])
    return h
```

```python
import math
from contextlib import ExitStack

import numpy as np
import concourse.bass as bass
import concourse.mybir as mybir
from concourse.bass_utils import run_bass_kernel_spmd

F32 = mybir.dt.float32
BF16 = mybir.dt.bfloat16
ALU = mybir.AluOpType
AF = mybir.ActivationFunctionType

S = 4096
D = 1024
DFF = 4096
PLE = 256
DIN = 2304
EPS = 1e-6
NCORES = 8
ARENA_BYTES = 206 * 1024
PATTERNS = (1, 4, 16)

STAGE = 4


class _Op:
    __slots__ = ("eng", "fn", "deps", "needs_inc", "inc_idx", "dma", "dma_idx", "rk", "wk")


class Sched:
    ENG = ("pe", "act", "dve", "pool", "sp")

    def __init__(self):
        self.ops = {e: [] for e in self.ENG}
        self.last_w = {}
        self.readers = {}
        self.chan_count = {}
        self.chan_last = {}

    def add(self, eng, fn, reads=(), writes=(), dma=None, noself=False):
        op = _Op()
        op.eng, op.fn, op.dma = eng, fn, dma
        op.needs_inc, op.inc_idx, op.dma_idx = False, 0, 0
        op.rk, op.wk = frozenset(reads), frozenset(writes)
        deps = []
        seen = set()

        def consider(d, raw_or_waw):
            if d is None or id(d) in seen:
                return
            same = d.eng == eng and d.dma is None and dma is None
            if same:
                if eng == "pe" or noself or not raw_or_waw:
                    return
            seen.add(id(d))
            deps.append(d)
            if d.dma is None:
                d.needs_inc = True

        for k in op.rk:
            consider(self.last_w.get(k), True)
        for k in op.wk:
            consider(self.last_w.get(k), True)
            for r in self.readers.get(k, ()):
                consider(r, False)
        op.deps = deps
        for k in op.rk:
            self.readers.setdefault(k, []).append(op)
        for k in op.wk:
            self.last_w[k] = op
            self.readers[k] = []
        if dma is not None:
            self.chan_count[dma] = self.chan_count.get(dma, 0) + 1
            op.dma_idx = self.chan_count[dma]
            self.chan_last[dma] = op
        self.ops[eng].append(op)
        return op

    def barrier(self):
        lasts = []
        for e in self.ENG:
            for op in reversed(self.ops[e]):
                if op.dma is None:
                    lasts.append(op)
                    break
        lasts += list(self.chan_last.values())
        for e in self.ENG:
            op = _Op()
            op.eng, op.fn, op.dma = e, (lambda en: en.nop()), None
            op.needs_inc, op.inc_idx, op.dma_idx = False, 0, 0
            op.rk = op.wk = frozenset()
            op.deps = []
            for d in lasts:
                if d.eng == e and d.dma is None:
                    continue
                op.deps.append(d)
                if d.dma is None:
                    d.needs_inc = True
            self.ops[e].append(op)
        self.last_w = {}
        self.readers = {}

    def finalize(self):
        for e in self.ENG:
            n = 0
            for op in self.ops[e]:
                if op.dma is None and op.needs_inc:
                    n += 1
                    op.inc_idx = n

    def emit(self, eng, eobj, eng_sems, dma_sems):
        waited = {}
        for op in self.ops[eng]:
            for d in op.deps:
                if d.dma is not None:
                    key, val, sem = ("d", d.dma), 16 * d.dma_idx, dma_sems[d.dma]
                else:
                    key, val, sem = ("e", d.eng), d.inc_idx, eng_sems[d.eng]
                if waited.get(key, 0) >= val:
                    continue
                waited[key] = val
                eobj.wait_ge(sem, val)
            ins = op.fn(eobj)
            if op.dma is not None:
                ins.then_inc(dma_sems[op.dma], 16)
            elif op.needs_inc:
                ins.then_inc(eng_sems[eng], 1)


def _t5_bucket_np(rel):
    num_buckets, max_distance = 32, 1024
    half = num_buckets // 2
    max_exact = half // 2
    rel = np.asarray(rel, dtype=np.int64)
    sign = np.where(rel > 0, half, 0)
    n = np.abs(rel)
    nf = np.maximum(n, 1).astype(np.float32)
    large = max_exact + (np.log(nf / np.float32(max_exact)) / np.float32(math.log(max_distance / max_exact))
                         * np.float32(half - max_exact)).astype(np.int32)
    large = np.minimum(large, half - 1)
    return sign + np.where(n < max_exact, n, large)


def _onehot_tables():
    oh = np.zeros((4, 32, 512), np.float32)
    j = np.arange(511)
    rel = 255 - j
    b = _t5_bucket_np(rel)
    ok = np.abs(rel) <= 128
    oh[0, b[ok], j[ok]] = 1.0
    for pi, d in enumerate(PATTERNS):
        j = np.arange(383)
        jr = 191 - j
        b = _t5_bucket_np(jr * d)
        ok = np.abs(jr) <= 64
        oh[1 + pi, b[ok], j[ok]] = 1.0
    return oh


def build_nc(stage=STAGE, debug=False, only_p3b=False):
    nc = bass.Bass("TRN2", target_bir_lowering=False)
    dt_in = lambda name, shape: nc.dram_tensor(name, list(shape), F32, kind="ExternalInput")
    x_d = dt_in("x", (S, D))
    p_d = dt_in("p", (S, PLE))
    tab_d = dt_in("rel_bias_table", (32, 16))
    g1_d = dt_in("g_pre_mix", (1, D))
    win_d = dt_in("w_in", (D, DIN))
    sink_d = dt_in("sink_a", (1, 8))
    gout_d = dt_in("g_out", (1, 1024))
    wo_d = dt_in("w_o", (D, D))
    g2_d = dt_in("g_post_mix", (1, D))
    g3_d = dt_in("g_pre_mlp", (1, D))
    wup_d = dt_in("w_up", (D, DFF))
    wdn_d = dt_in("w_down", (DFF, D))
    g4_d = dt_in("g_post_mlp", (1, D))
    wple_d = dt_in("w_ple_proj", (PLE, D))
    wg_d = dt_in("w_ple_gate", (D, D))
    bg_d = dt_in("b_ple_gate", (1, D))
    g5_d = dt_in("g_post_ple", (1, D))
    ident_d = dt_in("c_ident", (128, 128))
    jrev_d = dt_in("c_jrev", (128, 128))
    oh_d = dt_in("c_onehot", (4, 32, 512))
    y_d = nc.dram_tensor("y", [S, D], F32, kind="ExternalOutput")
    dbgkind = "ExternalOutput" if debug else "Internal"
    otd = nc.dram_tensor("otd", [D, S], BF16, kind=dbgkind)
    h1d = nc.dram_tensor("h1d", [S, D], F32, kind=dbgkind)
    gscr = nc.dram_tensor("gscr", [4 * 16, 512], BF16, kind="Internal")
    utd = nc.dram_tensor("utd", [D, S], BF16, kind="ExternalOutput") if debug else None

    sch = Sched()
    es = ExitStack()
    with es:
        arena = es.enter_context(nc.sbuf_tensor("arena", [128, ARENA_BYTES // 2], BF16))
        ps2 = [es.enter_context(nc.psum_tensor(f"ps{i}", [128, 1024], F32)) for i in range(4)]
        eng_sems = {e: es.enter_context(nc.semaphore(f"sem_{e}")) for e in Sched.ENG}
        dma_sems = {}

        def chan(name):
            if name not in dma_sems:
                dma_sems[name] = es.enter_context(nc.semaphore(f"dsem_{name}"))
            return name

        def view(off, shape, dt):
            n = int(np.prod(shape[1:]))
            assert off % 4 == 0
            if dt == BF16:
                assert off + 2 * n <= ARENA_BYTES, (off, shape)
                ap = arena[0:shape[0], off // 2: off // 2 + n]
            else:
                assert off + 4 * n <= ARENA_BYTES, (off, shape)
                ap = arena[0:shape[0], off // 2: off // 2 + 2 * n].bitcast(F32)
            if len(shape) == 3:
                ap = ap.rearrange("p (a b) -> p a b", a=shape[1])
            elif len(shape) == 4:
                ap = ap.rearrange("p (a b c) -> p a b c", a=shape[1], b=shape[2])
            return ap

        def bank(b):
            return ps2[b // 2][:, (b % 2) * 512:(b % 2) * 512 + 512]

        def bank_bf(b):
            return ps2[b // 2][:].bitcast(BF16)[:, (b % 2) * 1024:(b % 2) * 1024 + 1024]

        def pk(b):
            return ("ps", b)

        def dram_ap(t, offset, ap):
            return bass.AP(t.ap().tensor, offset, ap)

        capture = [None]

        def add(*a, **k):
            if capture[0] is not None:
                capture[0].append((a, k))
                return None
            return sch.add(*a, **k)

        def captured(fn):
            lst = []
            capture[0] = lst
            fn()
            capture[0] = None
            return lst

        def replay(lst, n):
            for _ in range(min(n, len(lst))):
                a, k = lst.pop(0)
                sch.add(*a, **k)

        def levelize(lst):
            lw, rd = {}, {}
            out = []
            for (a, k) in lst:
                eng = a[0]
                lvl = 0
                reads, writes = k.get("reads", ()), k.get("writes", ())
                for key in reads:
                    if key in lw:
                        le, ll = lw[key]
                        lvl = max(lvl, ll + (1 if le != eng else 0))
                for key in writes:
                    if key in lw:
                        le, ll = lw[key]
                        lvl = max(lvl, ll + (1 if le != eng else 0))
                    for (le, ll) in rd.get(key, ()):
                        lvl = max(lvl, ll + (1 if le != eng else 0))
                for key in reads:
                    rd.setdefault(key, []).append((eng, lvl))
                for key in writes:
                    lw[key] = (eng, lvl)
                    rd[key] = []
                out.append(lvl)
            return out

        def replay_levels(lst, lvls, upto):
            keep, keepl = [], []
            for item, l in zip(lst, lvls):
                if l <= upto:
                    sch.add(*item[0], **item[1])
                else:
                    keep.append(item)
                    keepl.append(l)
            lst[:] = keep
            lvls[:] = keepl

        CONST = ARENA_BYTES - 4096
        ident = view(CONST, [128, 128], BF16)
        jrev = view(CONST + 256, [128, 128], BF16)
        stat = view(CONST + 512, [128, 256], F32)
        esink = view(CONST + 1536, [128, 8], F32)
        gcol = view(CONST + 1568, [128, 8], F32)
        ones1 = view(CONST + 1600, [128, 2], BF16)
        expT = view(CONST + 1664, [32, 16], BF16)
        tabf = view(CONST + 1728, [32, 16], F32)
        ohb = view(CONST + 1792, [32, 512], BF16)
        grow = view(CONST + 2816, [16, 512], BF16)
        onesrow = view(CONST + 3840, [1, 128], BF16)

        add("pool", lambda e: e.dma_start(out=ident, in_=ident_d.ap()), writes=["ident"], dma=chan("c0"))
        add("pool", lambda e: e.dma_start(out=jrev, in_=jrev_d.ap()), writes=["jrev"], dma=chan("c1"))
        add("sp", lambda e: e.dma_start(out=esink, in_=dram_ap(sink_d, 0, [[0, 128], [1, 8]])),
            writes=["esink"], dma=chan("c2"))
        add("sp", lambda e: e.dma_start(out=gcol, in_=dram_ap(gout_d, 0, [[1, 128], [128, 8]]),
                                        allow_slow_non_contiguous=True),
            writes=["gcol"], dma=chan("c3"))
        add("sp", lambda e: e.dma_start(out=tabf, in_=tab_d.ap()), writes=["tabf"], dma=chan("c4"))
        add("act", lambda e: e.activation(out=esink, in_=esink, func=AF.Exp), reads=["esink"], writes=["esink"])
        add("act", lambda e: e.activation(out=expT, in_=tabf, func=AF.Exp), reads=["tabf"], writes=["expT"])
        add("dve", lambda e: e.memset(ones1, 1.0), writes=["ones1"])
        add("dve", lambda e: e.memset(onesrow, 1.0), writes=["onesrow"])

        UT = 0
        WIN = 65536
        WKD = WIN + 36864
        QT_ = WKD + 4096
        KT_ = QT_ + 8192
        VT_ = KT_ + 8192
        ACC = VT_ + 8192
        EBA = ACC + 32768
        EBB = EBA + 6144
        ERAW = EBB + 12288
        EE = ERAW + 2048
        VTS = EE + 2048
        OTP = VTS + 1536
        END2 = OTP + 8192
        assert END2 + 6144 <= CONST, END2

        uT = view(UT, [128, 8, S], BF16)
        win = view(WIN, [128, 8, DIN], BF16)
        wkd = view(WKD, [128, 8, 2, 128], BF16)
        QT = view(QT_, [128, S], BF16)
        KT = view(KT_, [128, S], BF16)
        VT = view(VT_, [128, S], BF16)
        acc = [view(ACC + 16384 * h, [128, S], F32) for h in range(2)]
        ebA = view(EBA, [128, 8, 384], BF16)
        ebB = view(EBB, [128, 3, 8, 256], BF16)
        eraw = [view(ERAW + 1024 * i, [128, 512], BF16) for i in range(2)]
        ee = [view(EE + 1024 * i, [128, 512], BF16) for i in range(2)]
        vts = [view(VTS + 384 * i, [128, 192], BF16) for i in range(3)]
        otp = view(OTP, [128, S], BF16)
        xs = [view(ACC + 4096 * i, [128, D], F32) for i in range(3)]
        ub = [view(ACC + 12288 + 2048 * i, [128, D], BF16) for i in range(2)]
        junk = view(ACC + 16384, [128, D], BF16)
        g1bc = view(ACC + 18432, [128, D], F32)
        rlt = [view(END2 + 2048 * i, [128, 512], F32) for i in range(2)]
        rawA = [view(END2 + 4096 + 1024 * i, [128, 512], BF16) for i in range(2)]
        hk = view(QT_, [128, 16, 384], BF16)

        for c in range(8):
            add("pool", lambda e, c=c: e.dma_start(out=win[:, c, :], in_=win_d.ap()[c * 128:(c + 1) * 128, :]),
                writes=[("win", c)], dma=chan("win"))
        for g in range(2):
            for rep in range(2):
                add("pool", lambda e, g=g, rep=rep: e.dma_start(
                    out=wkd[:, :, g, rep * 64:(rep + 1) * 64],
                    in_=dram_ap(win_d, 512 + 64 * g, [[DIN, 128], [128 * DIN, 8], [1, 64]])),
                    writes=[("wkd", g, rep)], dma=chan("wkd"))
        add("sp", lambda e: e.dma_start(out=g1bc, in_=dram_ap(g1_d, 0, [[0, 128], [1, D]])),
            writes=["g1bc"], dma=chan("c5"))

        def rstd_ops(ssq_col, out_col, n, keyin, keyout):
            add("act", lambda e: e.activation(out=out_col, in_=ssq_col, func=AF.Ln, scale=1.0 / n, bias=EPS),
                reads=[keyin], writes=[keyout])
            add("act", lambda e: e.activation(out=out_col, in_=out_col, func=AF.Exp, scale=-0.5),
                reads=[keyout], writes=[keyout])

        NT = S // 128
        for t in range(0 if only_p3b else NT):
            s3, s2 = t % 3, t % 2
            add("sp", lambda e, t=t, s3=s3: e.dma_start(out=xs[s3], in_=x_d.ap()[t * 128:(t + 1) * 128, :]),
                writes=[("xs", s3)], dma=chan(f"xs{s3}"))
            ssq = stat[:, t:t + 1]
            rs = stat[:, 32 + t:33 + t]
            add("act", lambda e, s3=s3, ssq=ssq: e.activation(out=junk, in_=xs[s3], func=AF.Square, accum_out=ssq),
                reads=[("xs", s3)], writes=["junk", ("ssq", t)], noself=True)
            rstd_ops(ssq, rs, D, ("ssq", t), ("rs", t))
            add("dve", lambda e, s3=s3, s2=s2, rs=rs: e.scalar_tensor_tensor(
                out=ub[s2], in0=xs[s3], scalar=rs, in1=g1bc, op0=ALU.mult, op1=ALU.mult),
                reads=[("xs", s3), ("rs", t), "g1bc"], writes=[("ub", s2)])
            pb = t % 2
            for c in range(8):
                add("pe", lambda e, c=c, s2=s2, pb=pb: e.transpose(
                    out=bank_bf(pb)[:, c * 128:(c + 1) * 128], in_=ub[s2][:, c * 128:(c + 1) * 128], identity=ident),
                    reads=[("ub", s2), "ident"], writes=[pk(pb)])
            src = bank_bf(pb).rearrange("p (a b) -> p a b", a=8)
            if t % 2 == 0:
                add("act", lambda e, t=t, src=src: e.activation(out=uT[:, :, t * 128:(t + 1) * 128], in_=src, func=AF.Copy),
                    reads=[pk(pb)], writes=[("uT", t)])
            else:
                add("dve", lambda e, t=t, src=src: e.tensor_copy(out=uT[:, :, t * 128:(t + 1) * 128], in_=src),
                    reads=[pk(pb)], writes=[("uT", t)])

        if debug:
            add("sp", lambda e: e.dma_start(
                out=dram_ap(utd, 0, [[S, 128], [128 * S, 8], [1, S]]), in_=uT),
                reads=[("uT", t) for t in range(NT)], writes=["utd"], dma=chan("dbg"))

        sch.barrier()

        if stage >= 2 and not only_p3b:
            for ti in range(4):
                add("pool", lambda e, ti=ti: e.dma_start(out=ohb, in_=oh_d.ap()[ti]), writes=["ohb"], dma=chan("oh"))
                add("pe", lambda e: e.matmul(bank(0)[0:16, :], lhsT=expT, rhs=ohb, start=True, stop=True),
                    reads=["expT", "ohb"], writes=[pk(0)])
                add("dve", lambda e: e.tensor_copy(out=grow, in_=bank(0)[0:16, :]), reads=[pk(0)], writes=["grow"])
                add("sp", lambda e, ti=ti: e.dma_start(out=gscr.ap()[ti * 16:(ti + 1) * 16, :], in_=grow),
                    reads=["grow"], writes=[("gscr", ti)], dma=chan("gs"))
            add("sp", lambda e: e.dma_start(out=hk[:, 0:8, :], in_=dram_ap(gscr, 0, [[1, 128], [512, 8], [1, 384]])),
                reads=[("gscr", 0)], writes=["hk"], dma=chan("hk"))
            for h in range(8):
                b = h % 2
                add("pe", lambda e, h=h, b=b: e.matmul(bank(b)[:, 0:384], lhsT=jrev, rhs=hk[:, h, :], start=True, stop=True),
                    reads=["jrev", "hk"], writes=[pk(b)])
                add("dve", lambda e, h=h, b=b: e.tensor_copy(out=ebA[:, h, :], in_=bank(b)[:, 0:384]),
                    reads=[pk(b)], writes=["ebA"], noself=True)
            for pi in range(3):
                add("sp", lambda e, pi=pi: e.dma_start(
                    out=hk[:, 0:8, 0:256], in_=dram_ap(gscr, ((1 + pi) * 16 + 8) * 512, [[1, 128], [512, 8], [1, 256]])),
                    reads=[("gscr", 1 + pi)], writes=["hk"], dma=chan("hk"))
                for h in range(8):
                    b = h % 2
                    add("pe", lambda e, h=h, b=b: e.matmul(bank(b)[:, 0:256], lhsT=jrev, rhs=hk[:, h, 0:256], start=True, stop=True),
                        reads=["jrev", "hk"], writes=[pk(b)])
                    add("dve", lambda e, h=h, b=b, pi=pi: e.tensor_copy(out=ebB[:, pi, h, :], in_=bank(b)[:, 0:256]),
                        reads=[pk(b)], writes=["ebB"], noself=True)
            sch.barrier()

            evac_rr = [0]

            def proj_fm(lhs_fn, dest, scale, dkey, wkeys):
                for tt in range(8):
                    b = 4 + (evac_rr[0] % 4)
                    evac_rr[0] += 1
                    for c in range(8):
                        add("pe", lambda e, c=c, tt=tt, b=b: e.matmul(
                            bank(b), lhsT=lhs_fn(c), rhs=uT[:, c, tt * 512:(tt + 1) * 512], start=(c == 0), stop=(c == 7)),
                            reads=wkeys, writes=[pk(b)])
                    if evac_rr[0] % 2 == 0:
                        add("act", lambda e, tt=tt, b=b: e.activation(out=dest[:, tt * 512:(tt + 1) * 512], in_=bank(b),
                                                                       func=AF.Identity, scale=scale),
                            reads=[pk(b)], writes=[(dkey, tt)])
                    else:
                        add("dve", lambda e, tt=tt, b=b: e.tensor_scalar(out=dest[:, tt * 512:(tt + 1) * 512], in0=bank(b),
                                                                          scalar1=scale, scalar2=None, op0=ALU.mult),
                            reads=[pk(b)], writes=[(dkey, tt)])

            def rkeys(name, lo, hi):
                return [(name, i) for i in range(lo // 512, (hi - 1) // 512 + 1)]

            for i in range(3):
                add("pool", lambda e, i=i: e.memset(vts[i][:, 64:128], 1.0), writes=[("vts", i)])

            allwin = [("win", c) for c in range(8)]
            job_ctr = [0]

            for j in range(4):
                g = j // 2
                proj_fm(lambda c, j=j: win[:, c, 128 * j:128 * j + 128], QT, 0.125, "QT", allwin)
                if j % 2 == 0:
                    proj_fm(lambda c, g=g: wkd[:, c, g, :], KT, 1.0, "KT", [("wkd", g, 0), ("wkd", g, 1)])
                if j == 0:
                    proj_fm(lambda c: win[:, c, 640:768], VT, 1.0, "VT", allwin)

                EA = [[view(EE + 1024 * par, [128, 512], BF16), view(ERAW + 1024 * par, [128, 512], BF16)] for par in range(2)]

                def front_a2(kb, j=j, g=g):
                    slot = job_ctr[0] % 3
                    par = job_ctr[0] % 2
                    job_ctr[0] += 1
                    qlo, qhi = max(kb - 1, 0), min(kb + 1, 31)
                    q0, q1 = qlo * 128, (qhi + 1) * 128
                    nq = q1 - q0
                    off = (qlo - (kb - 1)) * 128
                    add("pe", lambda e: e.transpose(out=bank_bf(6)[:, 0:128], in_=VT[:, kb * 128:(kb + 1) * 128], identity=ident),
                        reads=[("VT", kb // 4), "ident"], writes=[pk(6)])
                    add("act", lambda e: e.activation(out=vts[slot][:, 0:64], in_=bank_bf(6)[:, g * 64:(g + 1) * 64], func=AF.Copy),
                        reads=[pk(6)], writes=[("vts", slot)])
                    add("act", lambda e: e.activation(out=vts[slot][:, 128:192], in_=bank_bf(6)[:, g * 64:(g + 1) * 64], func=AF.Copy),
                        reads=[pk(6)], writes=[("vts", slot)], noself=True)
                    for hh in range(2):
                        h = 2 * j + hh
                        pr = slice(64 * hh, 64 * hh + 64)
                        sb = 4 + hh
                        add("pe", lambda e, pr=pr, sb=sb: e.matmul(
                            bank(sb)[:, 0:nq], lhsT=KT[pr, kb * 128:(kb + 1) * 128], rhs=QT[pr, q0:q1], start=True, stop=True),
                            reads=[("KT", kb // 4)] + rkeys("QT", q0, q1), writes=[pk(sb)])
                        add("act", lambda e, sb=sb, hh=hh: e.activation(out=rawA[hh][:, 0:nq], in_=bank(sb)[:, 0:nq], func=AF.Exp),
                            reads=[pk(sb)], writes=[("rawA", hh)])
                        add("dve", lambda e, hh=hh, h=h: e.tensor_tensor(
                            out=EA[par][hh][:, 0:nq], in0=rawA[hh][:, 0:nq], in1=ebA[:, h, off:off + nq], op=ALU.mult),
                            reads=[("rawA", hh), "ebA"], writes=[("EA", par, hh)])
                    return dict(kb=kb, slot=slot, par=par, q0=q0, q1=q1, nq=nq)

                started = set()

                def back_a(job, j=j):
                    kb, slot, par, q0, q1 = job["kb"], job["slot"], job["par"], job["q0"], job["q1"]
                    for hh in range(2):
                        h = 2 * j + hh
                        lhs = vts[slot][:, 0:128] if hh == 0 else vts[slot][:, 64:192]
                        Q0, Q1 = q0 // 512, (q1 - 1) // 512
                        for Q in range(Q0, Q1 + 1):
                            a0, a1 = max(q0, Q * 512), min(q1, (Q + 1) * 512)
                            ab = 2 * hh + (Q % 2)
                            first = (hh, Q) not in started
                            started.add((hh, Q))
                            add("pe", lambda e, lhs=lhs, ab=ab, a0=a0, a1=a1, first=first, Q=Q, hh=hh: e.matmul(
                                bank(ab)[:, a0 - Q * 512:a1 - Q * 512], lhsT=lhs,
                                rhs=EA[par][hh][:, a0 - q0:a1 - q0], start=first, stop=False, skip_group_check=True),
                                reads=[("vts", slot), ("EA", par, hh)], writes=[pk(ab)])
                        for Q in range(8):
                            if kb == min(4 * Q + 4, 31):
                                ab = 2 * hh + (Q % 2)
                                orow = slice(64 * hh, 64 * hh + 64)
                                lrow = slice(64 * (1 - hh), 64 * (1 - hh) + 64)
                                add("act", lambda e, ab=ab, lrow=lrow, orow=orow, h=h, hh=hh: e.activation(
                                    out=rlt[hh][orow, :], in_=bank(ab)[lrow, :], func=AF.Ln, bias=esink[orow, h:h + 1]),
                                    reads=[pk(ab), "esink"], writes=[("rlt", hh)])
                                add("act", lambda e, orow=orow, hh=hh: e.activation(
                                    out=rlt[hh][orow, :], in_=rlt[hh][orow, :], func=AF.Exp, scale=-1.0),
                                    reads=[("rlt", hh)], writes=[("rlt", hh)])
                                add("dve", lambda e, ab=ab, orow=orow, Q=Q, hh=hh: e.tensor_tensor(
                                    out=otp[orow, Q * 512:(Q + 1) * 512], in0=bank(ab)[orow, :], in1=rlt[hh][orow, :], op=ALU.mult),
                                    reads=[pk(ab), ("rlt", hh)], writes=[("otp", hh)])

                prev = None
                for kb in range(32):
                    job = front_a2(kb)
                    if prev is not None:
                        back_a(prev)
                    prev = job
                back_a(prev)
                add("sp", lambda e, j=j: e.dma_start(out=otd.ap()[128 * j:128 * j + 128, :], in_=otp),
                    reads=[("otp", 0), ("otp", 1)], writes=[("otd", j)], dma=chan("otp"))

            for j in range(4):
                proj_fm(lambda c, j=j: win[:, c, 768 + 128 * j:768 + 128 * j + 128], QT, 0.125, "QT", allwin)
                proj_fm(lambda c, j=j: win[:, c, 1280 + 128 * j:1280 + 128 * j + 128], KT, 1.0, "KT", allwin)
                proj_fm(lambda c, j=j: win[:, c, 1792 + 128 * j:1792 + 128 * j + 128], VT, 1.0, "VT", allwin)
                allq = [("QT", i) for i in range(8)]
                allk = [("KT", i) for i in range(8)]
                allv = [("VT", i) for i in range(8)]
                tile_ctr = [0]

                def front_b(pi, d, r, c, C, ls, j=j):
                    n = tile_ctr[0]
                    tile_ctr[0] += 1
                    slot, par = n % 3, n % 2
                    k0 = r + d * 128 * c
                    kcols = slice(k0, k0 + d * 127 + 1, d)
                    qs0, qs1 = max(128 * c - 64, 0), min(128 * c + 192, ls)
                    nq = qs1 - qs0
                    off = qs0 - (128 * c - 64)
                    qcols = slice(r + d * qs0, r + d * (qs1 - 1) + 1, d)
                    add("pe", lambda e: e.transpose(out=bank_bf(6)[:, 0:128], in_=VT[:, kcols], identity=ident),
                        reads=allv + ["ident"], writes=[pk(6)])
                    vdst = vts[slot].rearrange("p (a b) -> p a b", a=3)[:, 0:3:2, :]
                    vsrc = bank_bf(6)[:, 0:128].rearrange("p (a b) -> p a b", a=2)
                    add("act", lambda e: e.activation(out=vdst, in_=vsrc, func=AF.Copy),
                        reads=[pk(6)], writes=[("vts", slot)])
                    sb0 = 2 + 2 * par
                    for hh in range(2):
                        pr = slice(64 * hh, 64 * hh + 64)
                        sb = sb0 + hh
                        add("pe", lambda e, pr=pr, sb=sb: e.matmul(
                            bank(sb)[:, 0:nq], lhsT=KT[pr, kcols], rhs=QT[pr, qcols], start=True, stop=True),
                            reads=allk + allq, writes=[pk(sb)])
                    s_src = ps2[1 + par][:].rearrange("p (h c) -> p h c", h=2)[:, :, 0:nq]
                    er_v = eraw[par].rearrange("p (h c) -> p h c", h=2)[:, :, 0:nq]
                    ee_v = ee[par].rearrange("p (h c) -> p h c", h=2)[:, :, 0:nq]
                    eb_v = ebB[:, pi, 2 * j:2 * j + 2, off:off + nq]
                    add("act", lambda e: e.activation(out=er_v, in_=s_src, func=AF.Exp),
                        reads=[pk(sb0), pk(sb0 + 1)], writes=[("eraw", par)])
                    add("dve", lambda e: e.tensor_tensor(out=ee_v, in0=er_v, in1=eb_v, op=ALU.mult),
                        reads=[("eraw", par), "ebB"], writes=[("ee", par)])
                    return dict(pi=pi, d=d, r=r, c=c, C=C, ls=ls, slot=slot, par=par, qs0=qs0, qs1=qs1, nq=nq)

                accb = view(ACC, [128, 2, S], F32)

                def evac_b(job, cg):
                    d, r, ls, pi = job["d"], job["r"], job["ls"], job["pi"]
                    s0, s1 = max(128 * cg - 64, 0), min(128 * cg + 64, ls)
                    n = s1 - s0
                    ab = cg % 2
                    dst = accb[:, :, r + d * s0: r + d * (s1 - 1) + 1: d]
                    src = bank(ab)[:, 0:256].rearrange("p (h c) -> p h c", h=2)[:, :, 0:n]
                    if pi == 0:
                        add("act", lambda e: e.activation(out=dst, in_=src, func=AF.Copy),
                            reads=[pk(ab)], writes=[("acc", 0), ("acc", 1)], noself=True)
                    else:
                        add("dve", lambda e: e.tensor_tensor(out=dst, in0=src, in1=dst, op=ALU.add),
                            reads=[pk(ab), ("acc", 0), ("acc", 1)], writes=[("acc", 0), ("acc", 1)], noself=True)

                def back_b(job):
                    c, C, slot, par, qs0, nq = job["c"], job["C"], job["slot"], job["par"], job["qs0"], job["nq"]
                    s0, s1 = max(128 * c - 64, 0), 128 * c + 64
                    n1 = s1 - s0
                    n2 = nq - n1
                    for hh in range(2):
                        lhs = vts[slot][:, 0:128] if hh == 0 else vts[slot][:, 64:192]
                        ab1 = c % 2
                        add("pe", lambda e, lhs=lhs, hh=hh, ab1=ab1: e.matmul(
                            bank(ab1)[:, hh * 128:hh * 128 + n1], lhsT=lhs, rhs=ee[par][:, hh * 256:hh * 256 + n1],
                            start=(c == 0 and hh == 0), stop=False, skip_group_check=True),
                            reads=[("vts", slot), ("ee", par)], writes=[pk(ab1)])
                    for hh in range(2):
                        lhs = vts[slot][:, 0:128] if hh == 0 else vts[slot][:, 64:192]
                        ab2 = (c + 1) % 2
                        add("pe", lambda e, lhs=lhs, hh=hh, ab2=ab2: e.matmul(
                            bank(ab2)[:, hh * 128:hh * 128 + n2], lhsT=lhs, rhs=ee[par][:, hh * 256 + n1:hh * 256 + n1 + n2],
                            start=(hh == 0), stop=False, skip_group_check=True),
                            reads=[("vts", slot), ("ee", par)], writes=[pk(ab2)])
                    evac_b(job, c)
                    if c == C - 1:
                        evac_b(job, c + 1)

                prev = None
                for pi, d in enumerate(PATTERNS):
                    ls = S // d
                    C = ls // 128
                    for r in range(d):
                        for c in range(C):
                            job = front_b(pi, d, r, c, C, ls)
                            if prev is not None:
                                back_b(prev)
                            prev = job
                back_b(prev)
                for q in range(8):
                    cs = slice(q * 512, (q + 1) * 512)
                    for hh in range(2):
                        orow = slice(64 * hh, 64 * hh + 64)
                        lrow = slice(64 * (1 - hh), 64 * (1 - hh) + 64)
                        add("act", lambda e, hh=hh, orow=orow, lrow=lrow, cs=cs: e.activation(
                            out=rlt[hh][orow, :], in_=acc[hh][lrow, cs], func=AF.Ln),
                            reads=[("acc", hh)], writes=[("rlt", hh)])
                        add("act", lambda e, hh=hh, orow=orow: e.activation(
                            out=rlt[hh][orow, :], in_=rlt[hh][orow, :], func=AF.Exp, scale=-1.0),
                            reads=[("rlt", hh)], writes=[("rlt", hh)])
                        add("pool" if hh == 0 else "dve", lambda e, hh=hh, orow=orow, cs=cs: e.tensor_tensor(
                            out=otp[orow, cs], in0=acc[hh][orow, cs], in1=rlt[hh][orow, :], op=ALU.mult),
                            reads=[("acc", hh), ("rlt", hh)], writes=[("otp", hh)])
                add("sp", lambda e, j=j: e.dma_start(out=otd.ap()[512 + 128 * j:512 + 128 * j + 128, :], in_=otp),
                    reads=[("otp", 0), ("otp", 1)], writes=[("otd", 4 + j)], dma=chan("otp"))

            sch.barrier()

        if stage >= 3:
            ACTLIM = ARENA_BYTES - 4096 - 12288 - 65536 - 65536
            WO = 0
            OT_ = WO + 16384
            SQ_ = OT_ + 8192
            XH_ = SQ_ + 8192
            TMPA = XH_ + 12288
            MIX_ = TMPA + 4096
            G2_ = MIX_ + 8192
            JK_ = G2_ + 4096
            END3A = JK_ + 2048
            assert END3A <= ACTLIM, END3A
            wo = view(WO, [128, 8, D], BF16)
            ott = [view(OT_ + 4096 * i, [128, 8, 256], BF16) for i in range(2)]
            sqs = [view(SQ_ + 4096 * i, [128, 8, 256], BF16) for i in range(2)]
            xh = [view(XH_ + 4096 * i, [128, D], F32) for i in range(3)]
            tmpas = [view(TMPA, [128, D], F32)] * 2
            mixs = [view(MIX_ + 4096 * i, [128, D], F32) for i in range(2)]
            g2bc = view(G2_, [128, D], F32)
            junk3 = view(JK_, [128, D], BF16)
            WDN = ACTLIM
            WUP = WDN + 65536
            WG = WUP + 65536
            wdn = view(WDN, [128, 32, D], BF16)
            wup = view(WUP, [128, 8, DFF], BF16)
            wgA = view(WG, [128, 6, D], BF16)
            assert WG + 12288 == CONST, (WG, CONST)

            add("sp", lambda e: e.dma_start(out=g2bc, in_=dram_ap(g2_d, 0, [[0, 128], [1, D]])), writes=["g2bc"], dma=chan("c5"))
            stg = mixs
            for c in range(8):
                s_ = c % 2
                add("sp", lambda e, c=c, s_=s_: e.dma_start(out=stg[s_], in_=wo_d.ap()[c * 128:(c + 1) * 128, :]),
                    writes=[("mix", s_)], dma=chan(f"stg{s_}"))
                add("dve", lambda e, c=c, s_=s_: e.tensor_scalar(out=wo[:, c, :], in0=stg[s_], scalar1=gcol[:, c:c + 1],
                                                                  scalar2=None, op0=ALU.mult),
                    reads=[("mix", s_), "gcol"], writes=[("wo", c)])

            if stage >= 4:
                for c in range(8):
                    add("pool", lambda e, c=c: e.dma_start(out=wup[:, c, :], in_=wup_d.ap()[c * 128:(c + 1) * 128, :]),
                        writes=[("wup", c)], dma=chan("wup"))
                for c4 in range(8):
                    add("pool", lambda e, c4=c4: e.dma_start(
                        out=wdn[:, 4 * c4:4 * c4 + 4, :],
                        in_=dram_ap(wdn_d, 4 * c4 * 128 * D, [[D, 128], [128 * D, 4], [1, D]])),
                        writes=[("wdn", c4)], dma=chan("wdn"))
                add("pool", lambda e: e.dma_start(out=wgA, in_=dram_ap(wg_d, 0, [[D, 128], [128 * D, 6], [1, D]])),
                    writes=["wgA"], dma=chan("wgA"))

            NT2 = S // 256

            def ld_ott(tt):
                so = tt % 2
                add("sp", lambda e: e.dma_start(
                    out=ott[so], in_=dram_ap(otd, tt * 256, [[S, 128], [128 * S, 8], [1, 256]])),
                    reads=[("otd", i) for i in range(8)], writes=[("ott", so)], dma=chan(f"ott{so}"))

            def ld_x(t):
                sx = t % 3
                add("sp", lambda e: e.dma_start(out=xh[sx], in_=x_d.ap()[t * 128:(t + 1) * 128, :]),
                    writes=[("xh", sx)], dma=chan(f"xh{sx}"))

            if not only_p3b:
                ld_ott(0)
                ld_x(0)
            for tt in range(0 if only_p3b else NT2):
                so = tt % 2
                if tt + 1 < NT2:
                    ld_ott(tt + 1)
                sq = sqs[so]
                if tt == 0:
                    add("dve", lambda e, so=so: e.tensor_tensor(out=sqs[so], in0=ott[so], in1=ott[so], op=ALU.mult),
                        reads=[("ott", so)], writes=[("sq", so)])
                for sub in range(2):
                    t = 2 * tt + sub
                    sx = t % 3
                    tok = slice(sub * 128, sub * 128 + 128)
                    if t + 1 < S // 128:
                        ld_x(t + 1)
                    for grp in range(2):
                        for cc in range(4):
                            c = 4 * grp + cc
                            add("pe", lambda e, c=c, grp=grp, cc=cc, tok=tok, sq=sq: e.matmul(
                                bank(6)[:, grp:grp + 1], lhsT=sq[:, c, tok], rhs=ones1[:, 0:1],
                                start=(cc == 0), stop=(cc == 3), skip_group_check=True),
                                reads=[("sq", so), "ones1"], writes=[pk(6)])
                    ssqab = stat[:, 64 + 2 * t:66 + 2 * t]
                    rsab = stat[:, 128 + 2 * t:130 + 2 * t]
                    add("dve", lambda e, ssqab=ssqab: e.tensor_copy(out=ssqab, in_=bank(6)[:, 0:2]),
                        reads=[pk(6)], writes=[("ssqab", t)])
                    rstd_ops(ssqab, rsab, 512, ("ssqab", t), ("rsab", t))
                    for grp in range(2):
                        for n in range(2):
                            b = 2 * grp + n
                            for cc in range(4):
                                c = 4 * grp + cc
                                add("pe", lambda e, c=c, cc=cc, b=b, n=n, so=so, tok=tok: e.matmul(
                                    bank(b), lhsT=ott[so][:, c, tok], rhs=wo[:, c, n * 512:(n + 1) * 512],
                                    start=(cc == 0), stop=(cc == 3)),
                                    reads=[("ott", so), ("wo", c)], writes=[pk(b)])
                    mb = t % 2
                    tmpa, mix = tmpas[mb], mixs[mb]
                    add("act", lambda e, rsab=rsab, tmpa=tmpa: e.activation(out=tmpa, in_=ps2[0][:], func=AF.Identity, scale=rsab[:, 0:1]),
                        reads=[pk(0), pk(1), ("rsab", t)], writes=["tmpa"])
                    add("dve", lambda e, rsab=rsab, tmpa=tmpa, mix=mix: e.scalar_tensor_tensor(
                        out=mix, in0=ps2[1][:], scalar=rsab[:, 1:2], in1=tmpa, op0=ALU.mult, op1=ALU.add),
                        reads=[pk(2), pk(3), ("rsab", t), "tmpa"], writes=[("mix", mb)])
                    if sub == 0 and tt + 1 < NT2:
                        sn = (tt + 1) % 2
                        add("dve", lambda e, sn=sn: e.tensor_tensor(out=sqs[sn], in0=ott[sn], in1=ott[sn], op=ALU.mult),
                            reads=[("ott", sn)], writes=[("sq", sn)])
                    ssqm = stat[:, 192 + t:193 + t]
                    rsm = stat[:, 224 + t:225 + t]
                    add("act", lambda e, ssqm=ssqm, mix=mix: e.activation(out=junk3, in_=mix, func=AF.Square, accum_out=ssqm),
                        reads=[("mix", mb)], writes=["junk3", ("ssqm", t)], noself=True)
                    rstd_ops(ssqm, rsm, D, ("ssqm", t), ("rsm", t))
                    add("dve", lambda e, rsm=rsm, mix=mix: e.scalar_tensor_tensor(
                        out=mix, in0=mix, scalar=rsm, in1=g2bc, op0=ALU.mult, op1=ALU.mult),
                        reads=[("mix", mb), ("rsm", t), "g2bc"], writes=[("mix", mb)])
                    add("pool", lambda e, sx=sx, mix=mix: e.tensor_tensor(out=xh[sx], in0=mix, in1=xh[sx], op=ALU.add),
                        reads=[("mix", mb), ("xh", sx)], writes=[("xh", sx)])
                    add("pool", lambda e, t=t, sx=sx: e.dma_start(out=h1d.ap()[t * 128:(t + 1) * 128, :], in_=xh[sx]),
                        reads=[("xh", sx)], writes=[("h1d", t)], dma=chan(f"h1s{sx}"))
            sch.barrier()

        if stage >= 4:
            HX = 0
            HY = HX + 4096
            VB = HY + 8192
            H2B = VB + 4096
            VTT = H2B + 2048
            RR = VTT + 4096
            AT = RR + 2048
            T3 = AT + 1536
            H2T = T3 + 8192
            PBB = H2T + 2048
            PT = PBB + 1024
            GV = PT + 1024
            WPL = GV + 16384
            WG2 = WPL + 4096
            END3B = WG2 + 4096
            assert END3B <= ACTLIM, (END3B, ACTLIM)
            hx = view(HX, [128, D], F32)
            hy = [view(HY + 4096 * i, [128, D], F32) for i in range(2)]
            vbs = [view(VB + 2048 * i, [128, D], BF16) for i in range(2)]
            h2b = view(H2B, [128, D], BF16)
            vT = view(VTT, [128, 8, 256], BF16)
            rr = [view(RR + 1024 * i, [128, 256], F32) for i in range(2)]
            aT = [view(AT + 512 * i, [128, 256], BF16) for i in range(3)]
            t3 = [view(T3 + 4096 * i, [128, D], F32) for i in range(2)]
            h2Ts = [view(H2T, [128, 8, 128], BF16), view(GV + 12288 + 2048, [128, 8, 128], BF16)]
            pbb = view(PBB, [128, 2, 256], BF16)
            pT = view(PT, [128, 2, 256], BF16)
            g3bc = view(GV, [128, D], F32)
            g4bc = view(GV + 4096, [128, D], F32)
            g5bc = view(GV + 8192, [128, D], F32)
            bgrow = view(GV + 12288, [1, D], BF16)
            wple = view(WPL, [128, 2, D], BF16)
            wgB = view(WG2, [128, 2, D], BF16)

            def wg(c):
                return wgA[:, c, :] if c < 6 else wgB[:, c - 6, :]

            def wgk(c):
                return "wgA" if c < 6 else "wgB"

            add("pool", lambda e: e.dma_start(out=bgrow, in_=bg_d.ap()), writes=["bgrow"], dma=chan("bgr"))
            for nm, dst, src in (("g3bc", g3bc, g3_d), ("g4bc", g4bc, g4_d), ("g5bc", g5bc, g5_d)):
                add("sp", lambda e, dst=dst, src=src: e.dma_start(out=dst, in_=dram_ap(src, 0, [[0, 128], [1, D]])),
                    writes=[nm], dma=chan("c6" + nm))
            add("pool", lambda e: e.dma_start(out=wple, in_=dram_ap(wple_d, 0, [[D, 128], [128 * D, 2], [1, D]])),
                writes=["wple"], dma=chan("wpl"))
            add("pool", lambda e: e.dma_start(out=wgB, in_=dram_ap(wg_d, 6 * 128 * D, [[D, 128], [128 * D, 2], [1, D]])),
                writes=["wgB"], dma=chan("wgB"))

            NT2 = S // 256

            def xpre_elem(tt, sub):
                t = 2 * tt + sub
                add("sp", lambda e: e.dma_start(out=hx, in_=h1d.ap()[t * 128:(t + 1) * 128, :]),
                    reads=[("h1d", t)], writes=["hx"], dma=chan("hx"))
                ssq = stat[:, t:t + 1]
                rs = stat[:, 32 + t:33 + t]
                add("act", lambda e: e.activation(out=vbs[sub], in_=hx, func=AF.Square, accum_out=ssq),
                    reads=["hx"], writes=[("vb", sub), ("ssq", t)])
                rstd_ops(ssq, rs, D, ("ssq", t), ("rs", t))
                add("dve", lambda e: e.scalar_tensor_tensor(
                    out=vbs[sub], in0=hx, scalar=rs, in1=g3bc, op0=ALU.mult, op1=ALU.mult),
                    reads=["hx", ("rs", t), "g3bc"], writes=[("vb", sub)])

            def xpre_pe(tt, subs=(0, 1)):
                for sub in subs:
                    b = 7
                    for c in range(8):
                        add("pe", lambda e, c=c, b=b, sub=sub: e.transpose(
                            out=bank_bf(b)[:, c * 128:(c + 1) * 128], in_=vbs[sub][:, c * 128:(c + 1) * 128], identity=ident),
                            reads=[("vb", sub), "ident"], writes=[pk(b)])
                    src = bank_bf(b).rearrange("p (a b) -> p a b", a=8)
                    tok = slice(sub * 128, sub * 128 + 128)
                    if sub == 0:
                        add("act", lambda e, tok=tok, src=src: e.activation(out=vT[:, :, tok], in_=src, func=AF.Copy),
                            reads=[pk(b)], writes=[("vT", sub)])
                    else:
                        add("dve", lambda e, tok=tok, src=src: e.tensor_copy(out=vT[:, :, tok], in_=src),
                            reads=[pk(b)], writes=[("vT", sub)])

            def up(fc):
                ub_ = 4 + fc % 3
                for c in range(8):
                    add("pe", lambda e, c=c: e.matmul(
                        bank(ub_)[:, 0:256], lhsT=wup[:, c, fc * 128:(fc + 1) * 128], rhs=vT[:, c, :],
                        start=(c == 0), stop=(c == 7)),
                        reads=[("wup", c), ("vT", 0), ("vT", 1)], writes=[pk(ub_)])
                r_ = rr[fc % 2]
                a_ = aT[fc % 3]
                add("act", lambda e: e.activation(out=r_, in_=bank(ub_)[:, 0:256], func=AF.Relu),
                    reads=[pk(ub_)], writes=[("rr", fc % 2)])
                add("dve", lambda e: e.tensor_tensor(out=a_, in0=r_, in1=r_, op=ALU.mult),
                    reads=[("rr", fc % 2)], writes=[("aT", fc % 3)])

            def down(fc):
                a_ = aT[fc % 3]
                for s_ in range(2):
                    for n in range(2):
                        add("pe", lambda e, s_=s_, n=n: e.matmul(
                            bank(2 * s_ + n), lhsT=a_[:, s_ * 128:(s_ + 1) * 128], rhs=wdn[:, fc, n * 512:(n + 1) * 512],
                            start=(fc == 0), stop=(fc == 31)),
                            reads=[("aT", fc % 3), ("wdn", fc // 4)], writes=[pk(2 * s_ + n)])

            def y_head(tt):
                add("act", lambda e: e.activation(out=t3[0], in_=ps2[0][:], func=AF.Copy),
                    reads=[pk(0), pk(1)], writes=[("t3", 0)])
                add("dve", lambda e: e.tensor_copy(out=t3[1], in_=ps2[1][:]),
                    reads=[pk(2), pk(3)], writes=[("t3", 1)])

            def y_norm(tt, sub):
                t = 2 * tt + sub
                ssqf = stat[:, 64 + t:65 + t]
                rsf = stat[:, 96 + t:97 + t]
                add("act", lambda e: e.activation(out=h2b, in_=t3[sub], func=AF.Square, accum_out=ssqf),
                    reads=[("t3", sub)], writes=["h2b", ("ssqf", t)])
                rstd_ops(ssqf, rsf, D, ("ssqf", t), ("rsf", t))
                add("dve", lambda e: e.scalar_tensor_tensor(
                    out=t3[sub], in0=t3[sub], scalar=rsf, in1=g4bc, op0=ALU.mult, op1=ALU.mult),
                    reads=[("t3", sub), ("rsf", t), "g4bc"], writes=[("t3", sub)])

            def y_loads(tt):
                for sub in range(2):
                    t = 2 * tt + sub
                    add("sp", lambda e, t=t, sub=sub: e.dma_start(out=hy[sub], in_=h1d.ap()[t * 128:(t + 1) * 128, :]),
                        reads=[("h1d", t)], writes=[("hy", sub)], dma=chan(f"hy{sub}"))
                add("pool", lambda e: e.dma_start(
                    out=pbb, in_=dram_ap(p_d, tt * 256 * PLE, [[PLE, 128], [128 * PLE, 2], [1, PLE]])),
                    writes=["pbb"], dma=chan("pb"))

            def y_ptrans(tt):
                for sub in range(2):
                    for c2 in range(2):
                        add("pe", lambda e, sub=sub, c2=c2: e.transpose(
                            out=bank_bf(7)[:, (2 * sub + c2) * 128:(2 * sub + c2 + 1) * 128],
                            in_=pbb[:, sub, c2 * 128:(c2 + 1) * 128], identity=ident),
                            reads=["pbb", "ident"], writes=[pk(7)])
                for sub in range(2):
                    srcp = bank_bf(7)[:, sub * 256:(sub + 1) * 256].rearrange("p (c t) -> p c t", c=2)
                    add("dve", lambda e, sub=sub, srcp=srcp: e.tensor_copy(out=pT[:, :, sub * 128:(sub + 1) * 128], in_=srcp),
                        reads=[pk(7)], writes=[("pT", sub)])

            def y_s1(tt, sub):
                add("dve", lambda e: e.tensor_tensor(out=h2b, in0=t3[sub], in1=hy[sub], op=ALU.add),
                    reads=[("t3", sub), ("hy", sub)], writes=["h2b"])
                add("pool", lambda e: e.tensor_tensor(out=t3[sub], in0=t3[sub], in1=hy[sub], op=ALU.add),
                    reads=[("t3", sub), ("hy", sub)], writes=[("t3", sub)])

            def y_s2(tt, sub):
                for c in range(8):
                    add("pe", lambda e, c=c: e.transpose(out=bank_bf(7)[:, c * 128:(c + 1) * 128],
                                                         in_=h2b[:, c * 128:(c + 1) * 128], identity=ident),
                        reads=["h2b", "ident"], writes=[pk(7)])
                src = bank_bf(7).rearrange("p (a b) -> p a b", a=8)
                add("dve", lambda e: e.tensor_copy(out=h2Ts[sub], in_=src), reads=[pk(7)], writes=[("h2T", sub)])

            def y_half(tt, sub, n):
                tok = slice(sub * 128, sub * 128 + 128)
                cs = slice(n * 512, (n + 1) * 512)
                gpv = hy[sub][:, cs]
                k = ("hy", sub)
                for c in range(8):
                    add("pe", lambda e, c=c: e.matmul(
                        bank(7), lhsT=h2Ts[sub][:, c, :], rhs=wg(c)[:, cs], start=(c == 0), stop=False),
                        reads=[("h2T", sub), wgk(c)], writes=[pk(7)])
                add("pe", lambda e: e.matmul(bank(7), lhsT=onesrow, rhs=bgrow[:, cs], start=False, stop=True),
                    reads=["onesrow", "bgrow"], writes=[pk(7)])
                add("act", lambda e: e.activation(out=gpv, in_=bank(7), func=AF.Sigmoid), reads=[pk(7), k], writes=[k])
                for c2 in range(2):
                    add("pe", lambda e, c2=c2: e.matmul(
                        bank(7), lhsT=pT[:, c2, tok], rhs=wple[:, c2, cs], start=(c2 == 0), stop=(c2 == 1)),
                        reads=[("pT", sub), "wple"], writes=[pk(7)])
                add("dve", lambda e: e.tensor_tensor(out=gpv, in0=bank(7), in1=gpv, op=ALU.mult),
                    reads=[pk(7), k], writes=[k])

            def y_tail(tt, sub):
                t = 2 * tt + sub
                k = ("hy", sub)
                ssqg = stat[:, 128 + t:129 + t]
                rsg = stat[:, 160 + t:161 + t]
                add("act", lambda e: e.activation(out=h2b, in_=hy[sub], func=AF.Square, accum_out=ssqg),
                    reads=[k], writes=["h2b", ("ssqg", t)])
                rstd_ops(ssqg, rsg, D, ("ssqg", t), ("rsg", t))
                add("dve", lambda e: e.scalar_tensor_tensor(
                    out=hy[sub], in0=hy[sub], scalar=rsg, in1=g5bc, op0=ALU.mult, op1=ALU.mult),
                    reads=[k, ("rsg", t), "g5bc"], writes=[k])
                add("pool", lambda e: e.tensor_tensor(out=t3[sub], in0=t3[sub], in1=hy[sub], op=ALU.add),
                    reads=[("t3", sub), k], writes=[("t3", sub)])
                add("sp", lambda e: e.dma_start(out=y_d.ap()[t * 128:(t + 1) * 128, :], in_=t3[sub]),
                    reads=[("t3", sub)], writes=[("y", t)], dma=chan(f"ys{sub}"))

            def y_early(tt):
                y_loads(tt)
                y_norm(tt, 0)
                y_norm(tt, 1)
                y_ptrans(tt)

            def y_rest(tt):
                for sub in range(2):
                    y_s1(tt, sub)
                    y_s2(tt, sub)
                    y_half(tt, sub, 0)
                    y_half(tt, sub, 1)
                    y_tail(tt, sub)

            xpre_elem(0, 0)
            xpre_elem(0, 1)
            xpre_pe(0)
            for tt in range(NT2):
                ylist = captured(lambda: y_rest(tt - 1)) if tt >= 1 else []
                ylv = levelize(ylist)
                ymax = (max(ylv) + 1) if ylv else 0
                xlist = captured(lambda: (xpre_elem(tt + 1, 0), xpre_elem(tt + 1, 1))) if tt + 1 < NT2 else []
                xlv = levelize(xlist)
                xmax = (max(xlv) + 1) if xlv else 0
                if tt == 1 and debug:
                    print("Y levels:", ymax, "ops", len(ylist), "X levels:", xmax, len(xlist))
                up(0)
                up(1)
                for fc in range(32):
                    if fc + 2 < 32:
                        up(fc + 2)
                    if fc == 29 and tt + 1 < NT2:
                        xpre_pe(tt + 1, (0,))
                    if fc == 30 and tt + 1 < NT2:
                        xpre_pe(tt + 1, (1,))
                    down(fc)
                    if fc < 26:
                        if ylist:
                            replay_levels(ylist, ylv, ((fc + 1) * ymax) // 26 - 1)
                    if 12 <= fc < 26 and xlist:
                        replay_levels(xlist, xlv, ((fc - 11) * xmax) // 14 - 1)
                replay(ylist, len(ylist))
                replay(xlist, len(xlist))
                y_head(tt)
                y_early(tt)
            y_rest(NT2 - 1)

        sch.barrier()

        sch.finalize()
        with nc.Block() as block:
            @block.tensor
            def _(e):
                sch.emit("pe", e, eng_sems, dma_sems)

            @block.scalar
            def _(e):
                sch.emit("act", e, eng_sems, dma_sems)

            @block.vector
            def _(e):
                sch.emit("dve", e, eng_sems, dma_sems)

            @block.gpsimd
            def _(e):
                sch.emit("pool", e, eng_sems, dma_sems)

            @block.sync
            def _(e):
                sch.emit("sp", e, eng_sems, dma_sems)
    return nc


_CONSTS = None


def _consts():
    global _CONSTS
    if _CONSTS is None:
        _CONSTS = dict(
            c_ident=np.eye(128, dtype=np.float32),
            c_jrev=np.ascontiguousarray(np.eye(128, dtype=np.float32)[::-1]),
            c_onehot=_onehot_tables(),
        )
    return _CONSTS


def make_in_maps(x, p, rel_bias_table, g_pre_mix, w_in, sink_a, g_out_a, g_out_b, w_o,
                 g_post_mix, g_pre_mlp, w_up, w_down, g_post_mlp, w_ple_proj, w_ple_gate,
                 b_ple_gate, g_post_ple):
    f = lambda a: np.ascontiguousarray(np.asarray(a, dtype=np.float32))
    shared = dict(
        rel_bias_table=f(rel_bias_table), g_pre_mix=f(g_pre_mix[0:1]), w_in=f(w_in[0]), sink_a=f(sink_a[0:1]),
        g_out=f(np.concatenate([np.asarray(g_out_a[0]), np.asarray(g_out_b[0])])[None, :]),
        w_o=f(w_o[0]), g_post_mix=f(g_post_mix[0:1]), g_pre_mlp=f(g_pre_mlp[0:1]), w_up=f(w_up[0]),
        w_down=f(w_down[0]), g_post_mlp=f(g_post_mlp[0:1]), w_ple_proj=f(w_ple_proj[0]),
        w_ple_gate=f(w_ple_gate[0]), b_ple_gate=f(b_ple_gate[0:1]), g_post_ple=f(g_post_ple[0:1]),
    )
    shared.update(_consts())
    x = np.asarray(x)
    p = np.asarray(p)
    maps = []
    for b in range(NCORES):
        m = dict(shared)
        m["x"] = f(x[b])
        m["p"] = f(p[0, b])
        maps.append(m)
    return maps


def kernel(**inputs):
    in_maps = make_in_maps(**inputs)
    nc = build_nc()
    res = run_bass_kernel_spmd(nc, in_maps, core_ids=list(range(NCORES)))
    out = np.stack([np.asarray(r["y"], dtype=np.float32) for r in res.results], axis=0)
    return out
```

```python
import math
from contextlib import ExitStack

import numpy as np
import concourse.bass as bass
import concourse.mybir as mybir
from concourse.bass_utils import run_bass_kernel_spmd

F32 = mybir.dt.float32
BF16 = mybir.dt.bfloat16
ALU = mybir.AluOpType
AF = mybir.ActivationFunctionType

S = 4096
D = 1024
DFF = 4096
PLE = 256
DIN = 2304
EPS = 1e-6
NCORES = 8
ARENA_BYTES = 206 * 1024
PATTERNS = (1, 4, 16)

STAGE = 4


class _Op:
    __slots__ = ("eng", "fn", "deps", "needs_inc", "inc_idx", "dma", "dma_idx", "rk", "wk")


class Sched:
    ENG = ("pe", "act", "dve", "pool", "sp")

    def __init__(self):
        self.ops = {e: [] for e in self.ENG}
        self.last_w = {}
        self.readers = {}
        self.chan_count = {}
        self.chan_last = {}

    def add(self, eng, fn, reads=(), writes=(), dma=None, noself=False):
        op = _Op()
        op.eng, op.fn, op.dma = eng, fn, dma
        op.needs_inc, op.inc_idx, op.dma_idx = False, 0, 0
        op.rk, op.wk = frozenset(reads), frozenset(writes)
        deps = []
        seen = set()

        def consider(d, raw_or_waw):
            if d is None or id(d) in seen:
                return
            same = d.eng == eng and d.dma is None and dma is None
            if same:
                if eng == "pe" or noself or not raw_or_waw:
                    return
            seen.add(id(d))
            deps.append(d)
            if d.dma is None:
                d.needs_inc = True

        for k in op.rk:
            consider(self.last_w.get(k), True)
        for k in op.wk:
            consider(self.last_w.get(k), True)
            for r in self.readers.get(k, ()):
                consider(r, False)
        op.deps = deps
        for k in op.rk:
            self.readers.setdefault(k, []).append(op)
        for k in op.wk:
            self.last_w[k] = op
            self.readers[k] = []
        if dma is not None:
            self.chan_count[dma] = self.chan_count.get(dma, 0) + 1
            op.dma_idx = self.chan_count[dma]
            self.chan_last[dma] = op
        self.ops[eng].append(op)
        return op

    def barrier(self):
        lasts = []
        for e in self.ENG:
            for op in reversed(self.ops[e]):
                if op.dma is None:
                    lasts.append(op)
                    break
        lasts += list(self.chan_last.values())
        for e in self.ENG:
            op = _Op()
            op.eng, op.fn, op.dma = e, (lambda en: en.nop()), None
            op.needs_inc, op.inc_idx, op.dma_idx = False, 0, 0
            op.rk = op.wk = frozenset()
            op.deps = []
            for d in lasts:
                if d.eng == e and d.dma is None:
                    continue
                op.deps.append(d)
                if d.dma is None:
                    d.needs_inc = True
            self.ops[e].append(op)
        self.last_w = {}
        self.readers = {}

    def finalize(self):
        for e in self.ENG:
            n = 0
            for op in self.ops[e]:
                if op.dma is None and op.needs_inc:
                    n += 1
                    op.inc_idx = n

    def emit(self, eng, eobj, eng_sems, dma_sems):
        waited = {}
        for op in self.ops[eng]:
            for d in op.deps:
                if d.dma is not None:
                    key, val, sem = ("d", d.dma), 16 * d.dma_idx, dma_sems[d.dma]
                else:
                    key, val, sem = ("e", d.eng), d.inc_idx, eng_sems[d.eng]
                if waited.get(key, 0) >= val:
                    continue
                waited[key] = val
                eobj.wait_ge(sem, val)
            ins = op.fn(eobj)
            if op.dma is not None:
                ins.then_inc(dma_sems[op.dma], 16)
            elif op.needs_inc:
                ins.then_inc(eng_sems[eng], 1)


def _t5_bucket_np(rel):
    num_buckets, max_distance = 32, 1024
    half = num_buckets // 2
    max_exact = half // 2
    rel = np.asarray(rel, dtype=np.int64)
    sign = np.where(rel > 0, half, 0)
    n = np.abs(rel)
    nf = np.maximum(n, 1).astype(np.float32)
    large = max_exact + (np.log(nf / np.float32(max_exact)) / np.float32(math.log(max_distance / max_exact))
                         * np.float32(half - max_exact)).astype(np.int32)
    large = np.minimum(large, half - 1)
    return sign + np.where(n < max_exact, n, large)


def _onehot_tables():
    oh = np.zeros((4, 32, 512), np.float32)
    j = np.arange(511)
    rel = 255 - j
    b = _t5_bucket_np(rel)
    ok = np.abs(rel) <= 128
    oh[0, b[ok], j[ok]] = 1.0
    for pi, d in enumerate(PATTERNS):
        j = np.arange(383)
        jr = 191 - j
        b = _t5_bucket_np(jr * d)
        ok = np.abs(jr) <= 64
        oh[1 + pi, b[ok], j[ok]] = 1.0
    return oh


def build_nc(stage=STAGE, debug=False, only_p3b=False):
    nc = bass.Bass("TRN2", target_bir_lowering=False)
    dt_in = lambda name, shape: nc.dram_tensor(name, list(shape), F32, kind="ExternalInput")
    x_d = dt_in("x", (S, D))
    p_d = dt_in("p", (S, PLE))
    tab_d = dt_in("rel_bias_table", (32, 16))
    g1_d = dt_in("g_pre_mix", (1, D))
    win_d = dt_in("w_in", (D, DIN))
    sink_d = dt_in("sink_a", (1, 8))
    gout_d = dt_in("g_out", (1, 1024))
    wo_d = dt_in("w_o", (D, D))
    g2_d = dt_in("g_post_mix", (1, D))
    g3_d = dt_in("g_pre_mlp", (1, D))
    wup_d = dt_in("w_up", (D, DFF))
    wdn_d = dt_in("w_down", (DFF, D))
    g4_d = dt_in("g_post_mlp", (1, D))
    wple_d = dt_in("w_ple_proj", (PLE, D))
    wg_d = dt_in("w_ple_gate", (D, D))
    bg_d = dt_in("b_ple_gate", (1, D))
    g5_d = dt_in("g_post_ple", (1, D))
    ident_d = dt_in("c_ident", (128, 128))
    jrev_d = dt_in("c_jrev", (128, 128))
    oh_d = dt_in("c_onehot", (4, 32, 512))
    y_d = nc.dram_tensor("y", [S, D], F32, kind="ExternalOutput")
    dbgkind = "ExternalOutput" if debug else "Internal"
    otd = nc.dram_tensor("otd", [D, S], BF16, kind=dbgkind)
    h1d = nc.dram_tensor("h1d", [S, D], F32, kind=dbgkind)
    gscr = nc.dram_tensor("gscr", [4 * 16, 512], BF16, kind="Internal")
    utd = nc.dram_tensor("utd", [D, S], BF16, kind="ExternalOutput") if debug else None

    sch = Sched()
    es = ExitStack()
    with es:
        arena = es.enter_context(nc.sbuf_tensor("arena", [128, ARENA_BYTES // 2], BF16))
        ps2 = [es.enter_context(nc.psum_tensor(f"ps{i}", [128, 1024], F32)) for i in range(4)]
        eng_sems = {e: es.enter_context(nc.semaphore(f"sem_{e}")) for e in Sched.ENG}
        dma_sems = {}

        def chan(name):
            if name not in dma_sems:
                dma_sems[name] = es.enter_context(nc.semaphore(f"dsem_{name}"))
            return name

        def view(off, shape, dt):
            n = int(np.prod(shape[1:]))
            assert off % 4 == 0
            if dt == BF16:
                assert off + 2 * n <= ARENA_BYTES, (off, shape)
                ap = arena[0:shape[0], off // 2: off // 2 + n]
            else:
                assert off + 4 * n <= ARENA_BYTES, (off, shape)
                ap = arena[0:shape[0], off // 2: off // 2 + 2 * n].bitcast(F32)
            if len(shape) == 3:
                ap = ap.rearrange("p (a b) -> p a b", a=shape[1])
            elif len(shape) == 4:
                ap = ap.rearrange("p (a b c) -> p a b c", a=shape[1], b=shape[2])
            return ap

        def bank(b):
            return ps2[b // 2][:, (b % 2) * 512:(b % 2) * 512 + 512]

        def bank_bf(b):
            return ps2[b // 2][:].bitcast(BF16)[:, (b % 2) * 1024:(b % 2) * 1024 + 1024]

        def pk(b):
            return ("ps", b)

        def dram_ap(t, offset, ap):
            return bass.AP(t.ap().tensor, offset, ap)

        capture = [None]

        def add(*a, **k):
            if capture[0] is not None:
                capture[0].append((a, k))
                return None
            return sch.add(*a, **k)

        def captured(fn):
            lst = []
            capture[0] = lst
            fn()
            capture[0] = None
            return lst

        def replay(lst, n):
            for _ in range(min(n, len(lst))):
                a, k = lst.pop(0)
                sch.add(*a, **k)

        def levelize(lst):
            lw, rd = {}, {}
            out = []
            for (a, k) in lst:
                eng = a[0]
                lvl = 0
                reads, writes = k.get("reads", ()), k.get("writes", ())
                for key in reads:
                    if key in lw:
                        le, ll = lw[key]
                        lvl = max(lvl, ll + (1 if le != eng else 0))
                for key in writes:
                    if key in lw:
                        le, ll = lw[key]
                        lvl = max(lvl, ll + (1 if le != eng else 0))
                    for (le, ll) in rd.get(key, ()):
                        lvl = max(lvl, ll + (1 if le != eng else 0))
                for key in reads:
                    rd.setdefault(key, []).append((eng, lvl))
                for key in writes:
                    lw[key] = (eng, lvl)
                    rd[key] = []
                out.append(lvl)
            return out

        def replay_levels(lst, lvls, upto):
            keep, keepl = [], []
            for item, l in zip(lst, lvls):
                if l <= upto:
                    sch.add(*item[0], **item[1])
                else:
                    keep.append(item)
                    keepl.append(l)
            lst[:] = keep
            lvls[:] = keepl

        CONST = ARENA_BYTES - 4096
        ident = view(CONST, [128, 128], BF16)
        jrev = view(CONST + 256, [128, 128], BF16)
        stat = view(CONST + 512, [128, 256], F32)
        esink = view(CONST + 1536, [128, 8], F32)
        gcol = view(CONST + 1568, [128, 8], F32)
        ones1 = view(CONST + 1600, [128, 2], BF16)
        expT = view(CONST + 1664, [32, 16], BF16)
        tabf = view(CONST + 1728, [32, 16], F32)
        ohb = view(CONST + 1792, [32, 512], BF16)
        grow = view(CONST + 2816, [16, 512], BF16)
        onesrow = view(CONST + 3840, [1, 128], BF16)

        add("pool", lambda e: e.dma_start(out=ident, in_=ident_d.ap()), writes=["ident"], dma=chan("c0"))
        add("pool", lambda e: e.dma_start(out=jrev, in_=jrev_d.ap()), writes=["jrev"], dma=chan("c1"))
        add("sp", lambda e: e.dma_start(out=esink, in_=dram_ap(sink_d, 0, [[0, 128], [1, 8]])),
            writes=["esink"], dma=chan("c2"))
        add("sp", lambda e: e.dma_start(out=gcol, in_=dram_ap(gout_d, 0, [[1, 128], [128, 8]]),
                                        allow_slow_non_contiguous=True),
            writes=["gcol"], dma=chan("c3"))
        add("sp", lambda e: e.dma_start(out=tabf, in_=tab_d.ap()), writes=["tabf"], dma=chan("c4"))
        add("act", lambda e: e.activation(out=esink, in_=esink, func=AF.Exp), reads=["esink"], writes=["esink"])
        add("act", lambda e: e.activation(out=expT, in_=tabf, func=AF.Exp), reads=["tabf"], writes=["expT"])
        add("dve", lambda e: e.memset(ones1, 1.0), writes=["ones1"])
        add("dve", lambda e: e.memset(onesrow, 1.0), writes=["onesrow"])

        UT = 0
        WIN = 65536
        WKD = WIN + 36864
        QT_ = WKD + 4096
        KT_ = QT_ + 8192
        VT_ = KT_ + 8192
        ACC = VT_ + 8192
        EBA = ACC + 32768
        EBB = EBA + 6144
        ERAW = EBB + 12288
        EE = ERAW + 2048
        VTS = EE + 2048
        OTP = VTS + 1536
        END2 = OTP + 8192
        assert END2 + 6144 <= CONST, END2

        uT = view(UT, [128, 8, S], BF16)
        win = view(WIN, [128, 8, DIN], BF16)
        wkd = view(WKD, [128, 8, 2, 128], BF16)
        QT = view(QT_, [128, S], BF16)
        KT = view(KT_, [128, S], BF16)
        VT = view(VT_, [128, S], BF16)
        acc = [view(ACC + 16384 * h, [128, S], F32) for h in range(2)]
        ebA = view(EBA, [128, 8, 384], BF16)
        ebB = view(EBB, [128, 3, 8, 256], BF16)
        eraw = [view(ERAW + 1024 * i, [128, 512], BF16) for i in range(2)]
        ee = [view(EE + 1024 * i, [128, 512], BF16) for i in range(2)]
        vts = [view(VTS + 384 * i, [128, 192], BF16) for i in range(3)]
        otp = view(OTP, [128, S], BF16)
        xs = [view(ACC + 4096 * i, [128, D], F32) for i in range(3)]
        ub = [view(ACC + 12288 + 2048 * i, [128, D], BF16) for i in range(2)]
        junk = view(ACC + 16384, [128, D], BF16)
        g1bc = view(ACC + 18432, [128, D], F32)
        rlt = [view(END2 + 2048 * i, [128, 512], F32) for i in range(2)]
        rawA = [view(END2 + 4096 + 1024 * i, [128, 512], BF16) for i in range(2)]
        hk = view(QT_, [128, 16, 384], BF16)

        for c in range(8):
            add("pool", lambda e, c=c: e.dma_start(out=win[:, c, :], in_=win_d.ap()[c * 128:(c + 1) * 128, :]),
                writes=[("win", c)], dma=chan("win"))
        for g in range(2):
            for rep in range(2):
                add("pool", lambda e, g=g, rep=rep: e.dma_start(
                    out=wkd[:, :, g, rep * 64:(rep + 1) * 64],
                    in_=dram_ap(win_d, 512 + 64 * g, [[DIN, 128], [128 * DIN, 8], [1, 64]])),
                    writes=[("wkd", g, rep)], dma=chan("wkd"))
        add("sp", lambda e: e.dma_start(out=g1bc, in_=dram_ap(g1_d, 0, [[0, 128], [1, D]])),
            writes=["g1bc"], dma=chan("c5"))

        def rstd_ops(ssq_col, out_col, n, keyin, keyout):
            add("act", lambda e: e.activation(out=out_col, in_=ssq_col, func=AF.Ln, scale=1.0 / n, bias=EPS),
                reads=[keyin], writes=[keyout])
            add("act", lambda e: e.activation(out=out_col, in_=out_col, func=AF.Exp, scale=-0.5),
                reads=[keyout], writes=[keyout])

        NT = S // 128
        for t in range(0 if only_p3b else NT):
            s3, s2 = t % 3, t % 2
            add("sp", lambda e, t=t, s3=s3: e.dma_start(out=xs[s3], in_=x_d.ap()[t * 128:(t + 1) * 128, :]),
                writes=[("xs", s3)], dma=chan(f"xs{s3}"))
            ssq = stat[:, t:t + 1]
            rs = stat[:, 32 + t:33 + t]
            add("act", lambda e, s3=s3, ssq=ssq: e.activation(out=junk, in_=xs[s3], func=AF.Square, accum_out=ssq),
                reads=[("xs", s3)], writes=["junk", ("ssq", t)], noself=True)
            rstd_ops(ssq, rs, D, ("ssq", t), ("rs", t))
            add("dve", lambda e, s3=s3, s2=s2, rs=rs: e.scalar_tensor_tensor(
                out=ub[s2], in0=xs[s3], scalar=rs, in1=g1bc, op0=ALU.mult, op1=ALU.mult),
                reads=[("xs", s3), ("rs", t), "g1bc"], writes=[("ub", s2)])
            pb = t % 2
            for c in range(8):
                add("pe", lambda e, c=c, s2=s2, pb=pb: e.transpose(
                    out=bank_bf(pb)[:, c * 128:(c + 1) * 128], in_=ub[s2][:, c * 128:(c + 1) * 128], identity=ident),
                    reads=[("ub", s2), "ident"], writes=[pk(pb)])
            src = bank_bf(pb).rearrange("p (a b) -> p a b", a=8)
            if t % 2 == 0:
                add("act", lambda e, t=t, src=src: e.activation(out=uT[:, :, t * 128:(t + 1) * 128], in_=src, func=AF.Copy),
                    reads=[pk(pb)], writes=[("uT", t)])
            else:
                add("dve", lambda e, t=t, src=src: e.tensor_copy(out=uT[:, :, t * 128:(t + 1) * 128], in_=src),
                    reads=[pk(pb)], writes=[("uT", t)])

        if debug:
            add("sp", lambda e: e.dma_start(
                out=dram_ap(utd, 0, [[S, 128], [128 * S, 8], [1, S]]), in_=uT),
                reads=[("uT", t) for t in range(NT)], writes=["utd"], dma=chan("dbg"))

        sch.barrier()

        if stage >= 2 and not only_p3b:
            for ti in range(4):
                add("pool", lambda e, ti=ti: e.dma_start(out=ohb, in_=oh_d.ap()[ti]), writes=["ohb"], dma=chan("oh"))
                add("pe", lambda e: e.matmul(bank(0)[0:16, :], lhsT=expT, rhs=ohb, start=True, stop=True),
                    reads=["expT", "ohb"], writes=[pk(0)])
                add("dve", lambda e: e.tensor_copy(out=grow, in_=bank(0)[0:16, :]), reads=[pk(0)], writes=["grow"])
                add("sp", lambda e, ti=ti: e.dma_start(out=gscr.ap()[ti * 16:(ti + 1) * 16, :], in_=grow),
                    reads=["grow"], writes=[("gscr", ti)], dma=chan("gs"))
            add("sp", lambda e: e.dma_start(out=hk[:, 0:8, :], in_=dram_ap(gscr, 0, [[1, 128], [512, 8], [1, 384]])),
                reads=[("gscr", 0)], writes=["hk"], dma=chan("hk"))
            for h in range(8):
                b = h % 2
                add("pe", lambda e, h=h, b=b: e.matmul(bank(b)[:, 0:384], lhsT=jrev, rhs=hk[:, h, :], start=True, stop=True),
                    reads=["jrev", "hk"], writes=[pk(b)])
                add("dve", lambda e, h=h, b=b: e.tensor_copy(out=ebA[:, h, :], in_=bank(b)[:, 0:384]),
                    reads=[pk(b)], writes=["ebA"], noself=True)
            for pi in range(3):
                add("sp", lambda e, pi=pi: e.dma_start(
                    out=hk[:, 0:8, 0:256], in_=dram_ap(gscr, ((1 + pi) * 16 + 8) * 512, [[1, 128], [512, 8], [1, 256]])),
                    reads=[("gscr", 1 + pi)], writes=["hk"], dma=chan("hk"))
                for h in range(8):
                    b = h % 2
                    add("pe", lambda e, h=h, b=b: e.matmul(bank(b)[:, 0:256], lhsT=jrev, rhs=hk[:, h, 0:256], start=True, stop=True),
                        reads=["jrev", "hk"], writes=[pk(b)])
                    add("dve", lambda e, h=h, b=b, pi=pi: e.tensor_copy(out=ebB[:, pi, h, :], in_=bank(b)[:, 0:256]),
                        reads=[pk(b)], writes=["ebB"], noself=True)
            sch.barrier()

            evac_rr = [0]

            def proj_fm(lhs_fn, dest, scale, dkey, wkeys):
                for tt in range(8):
                    b = 4 + (evac_rr[0] % 4)
                    evac_rr[0] += 1
                    for c in range(8):
                        add("pe", lambda e, c=c, tt=tt, b=b: e.matmul(
                            bank(b), lhsT=lhs_fn(c), rhs=uT[:, c, tt * 512:(tt + 1) * 512], start=(c == 0), stop=(c == 7)),
                            reads=wkeys, writes=[pk(b)])
                    if evac_rr[0] % 2 == 0:
                        add("act", lambda e, tt=tt, b=b: e.activation(out=dest[:, tt * 512:(tt + 1) * 512], in_=bank(b),
                                                                       func=AF.Identity, scale=scale),
                            reads=[pk(b)], writes=[(dkey, tt)])
                    else:
                        add("dve", lambda e, tt=tt, b=b: e.tensor_scalar(out=dest[:, tt * 512:(tt + 1) * 512], in0=bank(b),
                                                                          scalar1=scale, scalar2=None, op0=ALU.mult),
                            reads=[pk(b)], writes=[(dkey, tt)])

            def rkeys(name, lo, hi):
                return [(name, i) for i in range(lo // 512, (hi - 1) // 512 + 1)]

            for i in range(3):
                add("pool", lambda e, i=i: e.memset(vts[i][:, 64:128], 1.0), writes=[("vts", i)])

            allwin = [("win", c) for c in range(8)]
            job_ctr = [0]

            for j in range(4):
                g = j // 2
                proj_fm(lambda c, j=j: win[:, c, 128 * j:128 * j + 128], QT, 0.125, "QT", allwin)
                if j % 2 == 0:
                    proj_fm(lambda c, g=g: wkd[:, c, g, :], KT, 1.0, "KT", [("wkd", g, 0), ("wkd", g, 1)])
                if j == 0:
                    proj_fm(lambda c: win[:, c, 640:768], VT, 1.0, "VT", allwin)

                EA = [[view(EE + 1024 * par, [128, 512], BF16), view(ERAW + 1024 * par, [128, 512], BF16)] for par in range(2)]

                def front_a2(kb, j=j, g=g):
                    slot = job_ctr[0] % 3
                    par = job_ctr[0] % 2
                    job_ctr[0] += 1
                    qlo, qhi = max(kb - 1, 0), min(kb + 1, 31)
                    q0, q1 = qlo * 128, (qhi + 1) * 128
                    nq = q1 - q0
                    off = (qlo - (kb - 1)) * 128
                    add("pe", lambda e: e.transpose(out=bank_bf(6)[:, 0:128], in_=VT[:, kb * 128:(kb + 1) * 128], identity=ident),
                        reads=[("VT", kb // 4), "ident"], writes=[pk(6)])
                    add("act", lambda e: e.activation(out=vts[slot][:, 0:64], in_=bank_bf(6)[:, g * 64:(g + 1) * 64], func=AF.Copy),
                        reads=[pk(6)], writes=[("vts", slot)])
                    add("act", lambda e: e.activation(out=vts[slot][:, 128:192], in_=bank_bf(6)[:, g * 64:(g + 1) * 64], func=AF.Copy),
                        reads=[pk(6)], writes=[("vts", slot)], noself=True)
                    for hh in range(2):
                        h = 2 * j + hh
                        pr = slice(64 * hh, 64 * hh + 64)
                        sb = 4 + hh
                        add("pe", lambda e, pr=pr, sb=sb: e.matmul(
                            bank(sb)[:, 0:nq], lhsT=KT[pr, kb * 128:(kb + 1) * 128], rhs=QT[pr, q0:q1], start=True, stop=True),
                            reads=[("KT", kb // 4)] + rkeys("QT", q0, q1), writes=[pk(sb)])
                        add("act", lambda e, sb=sb, hh=hh: e.activation(out=rawA[hh][:, 0:nq], in_=bank(sb)[:, 0:nq], func=AF.Exp),
                            reads=[pk(sb)], writes=[("rawA", hh)])
                        add("dve", lambda e, hh=hh, h=h: e.tensor_tensor(
                            out=EA[par][hh][:, 0:nq], in0=rawA[hh][:, 0:nq], in1=ebA[:, h, off:off + nq], op=ALU.mult),
                            reads=[("rawA", hh), "ebA"], writes=[("EA", par, hh)])
                    return dict(kb=kb, slot=slot, par=par, q0=q0, q1=q1, nq=nq)

                started = set()

                def back_a(job, j=j):
                    kb, slot, par, q0, q1 = job["kb"], job["slot"], job["par"], job["q0"], job["q1"]
                    for hh in range(2):
                        h = 2 * j + hh
                        lhs = vts[slot][:, 0:128] if hh == 0 else vts[slot][:, 64:192]
                        Q0, Q1 = q0 // 512, (q1 - 1) // 512
                        for Q in range(Q0, Q1 + 1):
                            a0, a1 = max(q0, Q * 512), min(q1, (Q + 1) * 512)
                            ab = 2 * hh + (Q % 2)
                            first = (hh, Q) not in started
                            started.add((hh, Q))
                            add("pe", lambda e, lhs=lhs, ab=ab, a0=a0, a1=a1, first=first, Q=Q, hh=hh: e.matmul(
                                bank(ab)[:, a0 - Q * 512:a1 - Q * 512], lhsT=lhs,
                                rhs=EA[par][hh][:, a0 - q0:a1 - q0], start=first, stop=False, skip_group_check=True),
                                reads=[("vts", slot), ("EA", par, hh)], writes=[pk(ab)])
                        for Q in range(8):
                            if kb == min(4 * Q + 4, 31):
                                ab = 2 * hh + (Q % 2)
                                orow = slice(64 * hh, 64 * hh + 64)
                                lrow = slice(64 * (1 - hh), 64 * (1 - hh) + 64)
                                add("act", lambda e, ab=ab, lrow=lrow, orow=orow, h=h, hh=hh: e.activation(
                                    out=rlt[hh][orow, :], in_=bank(ab)[lrow, :], func=AF.Ln, bias=esink[orow, h:h + 1]),
                                    reads=[pk(ab), "esink"], writes=[("rlt", hh)])
                                add("act", lambda e, orow=orow, hh=hh: e.activation(
                                    out=rlt[hh][orow, :], in_=rlt[hh][orow, :], func=AF.Exp, scale=-1.0),
                                    reads=[("rlt", hh)], writes=[("rlt", hh)])
                                add("dve", lambda e, ab=ab, orow=orow, Q=Q, hh=hh: e.tensor_tensor(
                                    out=otp[orow, Q * 512:(Q + 1) * 512], in0=bank(ab)[orow, :], in1=rlt[hh][orow, :], op=ALU.mult),
                                    reads=[pk(ab), ("rlt", hh)], writes=[("otp", hh)])

                prev = None
                for kb in range(32):
                    job = front_a2(kb)
                    if prev is not None:
                        back_a(prev)
                    prev = job
                back_a(prev)
                add("sp", lambda e, j=j: e.dma_start(out=otd.ap()[128 * j:128 * j + 128, :], in_=otp),
                    reads=[("otp", 0), ("otp", 1)], writes=[("otd", j)], dma=chan("otp"))

            for j in range(4):
                proj_fm(lambda c, j=j: win[:, c, 768 + 128 * j:768 + 128 * j + 128], QT, 0.125, "QT", allwin)
                proj_fm(lambda c, j=j: win[:, c, 1280 + 128 * j:1280 + 128 * j + 128], KT, 1.0, "KT", allwin)
                proj_fm(lambda c, j=j: win[:, c, 1792 + 128 * j:1792 + 128 * j + 128], VT, 1.0, "VT", allwin)
                allq = [("QT", i) for i in range(8)]
                allk = [("KT", i) for i in range(8)]
                allv = [("VT", i) for i in range(8)]
                tile_ctr = [0]

                def front_b(pi, d, r, c, C, ls, j=j):
                    n = tile_ctr[0]
                    tile_ctr[0] += 1
                    slot, par = n % 3, n % 2
                    k0 = r + d * 128 * c
                    kcols = slice(k0, k0 + d * 127 + 1, d)
                    qs0, qs1 = max(128 * c - 64, 0), min(128 * c + 192, ls)
                    nq = qs1 - qs0
                    off = qs0 - (128 * c - 64)
                    qcols = slice(r + d * qs0, r + d * (qs1 - 1) + 1, d)
                    add("pe", lambda e: e.transpose(out=bank_bf(6)[:, 0:128], in_=VT[:, kcols], identity=ident),
                        reads=allv + ["ident"], writes=[pk(6)])
                    vdst = vts[slot].rearrange("p (a b) -> p a b", a=3)[:, 0:3:2, :]
                    vsrc = bank_bf(6)[:, 0:128].rearrange("p (a b) -> p a b", a=2)
                    add("act", lambda e: e.activation(out=vdst, in_=vsrc, func=AF.Copy),
                        reads=[pk(6)], writes=[("vts", slot)])
                    for hh in range(2):
                        h = 2 * j + hh
                        pr = slice(64 * hh, 64 * hh + 64)
                        sb = 2 + 2 * par + hh
                        add("pe", lambda e, pr=pr, sb=sb: e.matmul(
                            bank(sb)[:, 0:nq], lhsT=KT[pr, kcols], rhs=QT[pr, qcols], start=True, stop=True),
                            reads=allk + allq, writes=[pk(sb)])
                        add("act", lambda e, sb=sb, hh=hh: e.activation(
                            out=eraw[par][:, hh * 256:hh * 256 + nq], in_=bank(sb)[:, 0:nq], func=AF.Exp),
                            reads=[pk(sb)], writes=[("eraw", par, hh)])
                        add("dve", lambda e, hh=hh, h=h: e.tensor_tensor(
                            out=ee[par][:, hh * 256:hh * 256 + nq], in0=eraw[par][:, hh * 256:hh * 256 + nq],
                            in1=ebB[:, pi, h, off:off + nq], op=ALU.mult),
                            reads=[("eraw", par, hh), "ebB"], writes=[("ee", par, hh)])
                    return dict(pi=pi, d=d, r=r, c=c, C=C, ls=ls, slot=slot, par=par, qs0=qs0, qs1=qs1, nq=nq)

                def evac_b(job, cg, hh, colbase):
                    d, r, ls, pi = job["d"], job["r"], job["ls"], job["pi"]
                    s0, s1 = max(128 * cg - 64, 0), min(128 * cg + 64, ls)
                    n = s1 - s0
                    ab = cg % 2
                    dst = acc[hh][:, r + d * s0: r + d * (s1 - 1) + 1: d]
                    src = bank(ab)[:, colbase:colbase + n]
                    if pi == 0:
                        add("act", lambda e: e.activation(out=dst, in_=src, func=AF.Copy),
                            reads=[pk(ab)], writes=[("acc", hh)], noself=True)
                    else:
                        add("dve", lambda e: e.tensor_tensor(out=dst, in0=src, in1=dst, op=ALU.add),
                            reads=[pk(ab), ("acc", hh)], writes=[("acc", hh)], noself=True)

                def back_b(job):
                    c, C, slot, par, qs0, nq = job["c"], job["C"], job["slot"], job["par"], job["qs0"], job["nq"]
                    s0, s1 = max(128 * c - 64, 0), 128 * c + 64
                    n1 = s1 - s0
                    n2 = nq - n1
                    for hh in range(2):
                        lhs = vts[slot][:, 0:128] if hh == 0 else vts[slot][:, 64:192]
                        ab1 = c % 2
                        add("pe", lambda e, lhs=lhs, hh=hh, ab1=ab1: e.matmul(
                            bank(ab1)[:, hh * 128:hh * 128 + n1], lhsT=lhs, rhs=ee[par][:, hh * 256:hh * 256 + n1],
                            start=(c == 0 and hh == 0), stop=False, skip_group_check=True),
                            reads=[("vts", slot), ("ee", par, hh)], writes=[pk(ab1)])
                    for hh in range(2):
                        lhs = vts[slot][:, 0:128] if hh == 0 else vts[slot][:, 64:192]
                        ab2 = (c + 1) % 2
                        add("pe", lambda e, lhs=lhs, hh=hh, ab2=ab2: e.matmul(
                            bank(ab2)[:, hh * 128:hh * 128 + n2], lhsT=lhs, rhs=ee[par][:, hh * 256 + n1:hh * 256 + n1 + n2],
                            start=(hh == 0), stop=False, skip_group_check=True),
                            reads=[("vts", slot), ("ee", par, hh)], writes=[pk(ab2)])
                    for hh in range(2):
                        evac_b(job, c, hh, hh * 128)
                    if c == C - 1:
                        for hh in range(2):
                            evac_b(job, c + 1, hh, hh * 128)

                prev = None
                for pi, d in enumerate(PATTERNS):
                    ls = S // d
                    C = ls // 128
                    for r in range(d):
                        for c in range(C):
                            job = front_b(pi, d, r, c, C, ls)
                            if prev is not None:
                                back_b(prev)
                            prev = job
                back_b(prev)
                for q in range(8):
                    cs = slice(q * 512, (q + 1) * 512)
                    for hh in range(2):
                        orow = slice(64 * hh, 64 * hh + 64)
                        lrow = slice(64 * (1 - hh), 64 * (1 - hh) + 64)
                        add("act", lambda e, hh=hh, orow=orow, lrow=lrow, cs=cs: e.activation(
                            out=rlt[hh][orow, :], in_=acc[hh][lrow, cs], func=AF.Ln),
                            reads=[("acc", hh)], writes=[("rlt", hh)])
                        add("act", lambda e, hh=hh, orow=orow: e.activation(
                            out=rlt[hh][orow, :], in_=rlt[hh][orow, :], func=AF.Exp, scale=-1.0),
                            reads=[("rlt", hh)], writes=[("rlt", hh)])
                        add("pool" if hh == 0 else "dve", lambda e, hh=hh, orow=orow, cs=cs: e.tensor_tensor(
                            out=otp[orow, cs], in0=acc[hh][orow, cs], in1=rlt[hh][orow, :], op=ALU.mult),
                            reads=[("acc", hh), ("rlt", hh)], writes=[("otp", hh)])
                add("sp", lambda e, j=j: e.dma_start(out=otd.ap()[512 + 128 * j:512 + 128 * j + 128, :], in_=otp),
                    reads=[("otp", 0), ("otp", 1)], writes=[("otd", 4 + j)], dma=chan("otp"))

            sch.barrier()

        if stage >= 3:
            ACTLIM = ARENA_BYTES - 4096 - 12288 - 65536 - 65536
            WO = 0
            OT_ = WO + 16384
            SQ_ = OT_ + 8192
            XH_ = SQ_ + 8192
            TMPA = XH_ + 12288
            MIX_ = TMPA + 4096
            G2_ = MIX_ + 8192
            JK_ = G2_ + 4096
            END3A = JK_ + 2048
            assert END3A <= ACTLIM, END3A
            wo = view(WO, [128, 8, D], BF16)
            ott = [view(OT_ + 4096 * i, [128, 8, 256], BF16) for i in range(2)]
            sqs = [view(SQ_ + 4096 * i, [128, 8, 256], BF16) for i in range(2)]
            xh = [view(XH_ + 4096 * i, [128, D], F32) for i in range(3)]
            tmpas = [view(TMPA, [128, D], F32)] * 2
            mixs = [view(MIX_ + 4096 * i, [128, D], F32) for i in range(2)]
            g2bc = view(G2_, [128, D], F32)
            junk3 = view(JK_, [128, D], BF16)
            WDN = ACTLIM
            WUP = WDN + 65536
            WG = WUP + 65536
            wdn = view(WDN, [128, 32, D], BF16)
            wup = view(WUP, [128, 8, DFF], BF16)
            wgA = view(WG, [128, 6, D], BF16)
            assert WG + 12288 == CONST, (WG, CONST)

            add("sp", lambda e: e.dma_start(out=g2bc, in_=dram_ap(g2_d, 0, [[0, 128], [1, D]])), writes=["g2bc"], dma=chan("c5"))
            stg = mixs
            for c in range(8):
                s_ = c % 2
                add("sp", lambda e, c=c, s_=s_: e.dma_start(out=stg[s_], in_=wo_d.ap()[c * 128:(c + 1) * 128, :]),
                    writes=[("mix", s_)], dma=chan(f"stg{s_}"))
                add("dve", lambda e, c=c, s_=s_: e.tensor_scalar(out=wo[:, c, :], in0=stg[s_], scalar1=gcol[:, c:c + 1],
                                                                  scalar2=None, op0=ALU.mult),
                    reads=[("mix", s_), "gcol"], writes=[("wo", c)])

            if stage >= 4:
                for c in range(8):
                    add("pool", lambda e, c=c: e.dma_start(out=wup[:, c, :], in_=wup_d.ap()[c * 128:(c + 1) * 128, :]),
                        writes=[("wup", c)], dma=chan("wup"))
                for c4 in range(8):
                    add("pool", lambda e, c4=c4: e.dma_start(
                        out=wdn[:, 4 * c4:4 * c4 + 4, :],
                        in_=dram_ap(wdn_d, 4 * c4 * 128 * D, [[D, 128], [128 * D, 4], [1, D]])),
                        writes=[("wdn", c4)], dma=chan("wdn"))
                add("pool", lambda e: e.dma_start(out=wgA, in_=dram_ap(wg_d, 0, [[D, 128], [128 * D, 6], [1, D]])),
                    writes=["wgA"], dma=chan("wgA"))

            NT2 = S // 256

            def ld_ott(tt):
                so = tt % 2
                add("sp", lambda e: e.dma_start(
                    out=ott[so], in_=dram_ap(otd, tt * 256, [[S, 128], [128 * S, 8], [1, 256]])),
                    reads=[("otd", i) for i in range(8)], writes=[("ott", so)], dma=chan(f"ott{so}"))

            def ld_x(t):
                sx = t % 3
                add("sp", lambda e: e.dma_start(out=xh[sx], in_=x_d.ap()[t * 128:(t + 1) * 128, :]),
                    writes=[("xh", sx)], dma=chan(f"xh{sx}"))

            if not only_p3b:
                ld_ott(0)
                ld_x(0)
            for tt in range(0 if only_p3b else NT2):
                so = tt % 2
                if tt + 1 < NT2:
                    ld_ott(tt + 1)
                sq = sqs[so]
                if tt == 0:
                    add("dve", lambda e, so=so: e.tensor_tensor(out=sqs[so], in0=ott[so], in1=ott[so], op=ALU.mult),
                        reads=[("ott", so)], writes=[("sq", so)])
                for sub in range(2):
                    t = 2 * tt + sub
                    sx = t % 3
                    tok = slice(sub * 128, sub * 128 + 128)
                    if t + 1 < S // 128:
                        ld_x(t + 1)
                    for grp in range(2):
                        for cc in range(4):
                            c = 4 * grp + cc
                            add("pe", lambda e, c=c, grp=grp, cc=cc, tok=tok, sq=sq: e.matmul(
                                bank(6)[:, grp:grp + 1], lhsT=sq[:, c, tok], rhs=ones1[:, 0:1],
                                start=(cc == 0), stop=(cc == 3), skip_group_check=True),
                                reads=[("sq", so), "ones1"], writes=[pk(6)])
                    ssqab = stat[:, 64 + 2 * t:66 + 2 * t]
                    rsab = stat[:, 128 + 2 * t:130 + 2 * t]
                    add("dve", lambda e, ssqab=ssqab: e.tensor_copy(out=ssqab, in_=bank(6)[:, 0:2]),
                        reads=[pk(6)], writes=[("ssqab", t)])
                    rstd_ops(ssqab, rsab, 512, ("ssqab", t), ("rsab", t))
                    for grp in range(2):
                        for n in range(2):
                            b = 2 * grp + n
                            for cc in range(4):
                                c = 4 * grp + cc
                                add("pe", lambda e, c=c, cc=cc, b=b, n=n, so=so, tok=tok: e.matmul(
                                    bank(b), lhsT=ott[so][:, c, tok], rhs=wo[:, c, n * 512:(n + 1) * 512],
                                    start=(cc == 0), stop=(cc == 3)),
                                    reads=[("ott", so), ("wo", c)], writes=[pk(b)])
                    mb = t % 2
                    tmpa, mix = tmpas[mb], mixs[mb]
                    add("act", lambda e, rsab=rsab, tmpa=tmpa: e.activation(out=tmpa, in_=ps2[0][:], func=AF.Identity, scale=rsab[:, 0:1]),
                        reads=[pk(0), pk(1), ("rsab", t)], writes=["tmpa"])
                    add("dve", lambda e, rsab=rsab, tmpa=tmpa, mix=mix: e.scalar_tensor_tensor(
                        out=mix, in0=ps2[1][:], scalar=rsab[:, 1:2], in1=tmpa, op0=ALU.mult, op1=ALU.add),
                        reads=[pk(2), pk(3), ("rsab", t), "tmpa"], writes=[("mix", mb)])
                    if sub == 0 and tt + 1 < NT2:
                        sn = (tt + 1) % 2
                        add("dve", lambda e, sn=sn: e.tensor_tensor(out=sqs[sn], in0=ott[sn], in1=ott[sn], op=ALU.mult),
                            reads=[("ott", sn)], writes=[("sq", sn)])
                    ssqm = stat[:, 192 + t:193 + t]
                    rsm = stat[:, 224 + t:225 + t]
                    add("act", lambda e, ssqm=ssqm, mix=mix: e.activation(out=junk3, in_=mix, func=AF.Square, accum_out=ssqm),
                        reads=[("mix", mb)], writes=["junk3", ("ssqm", t)], noself=True)
                    rstd_ops(ssqm, rsm, D, ("ssqm", t), ("rsm", t))
                    add("dve", lambda e, rsm=rsm, mix=mix: e.scalar_tensor_tensor(
                        out=mix, in0=mix, scalar=rsm, in1=g2bc, op0=ALU.mult, op1=ALU.mult),
                        reads=[("mix", mb), ("rsm", t), "g2bc"], writes=[("mix", mb)])
                    add("pool", lambda e, sx=sx, mix=mix: e.tensor_tensor(out=xh[sx], in0=mix, in1=xh[sx], op=ALU.add),
                        reads=[("mix", mb), ("xh", sx)], writes=[("xh", sx)])
                    add("pool", lambda e, t=t, sx=sx: e.dma_start(out=h1d.ap()[t * 128:(t + 1) * 128, :], in_=xh[sx]),
                        reads=[("xh", sx)], writes=[("h1d", t)], dma=chan(f"h1s{sx}"))
            sch.barrier()

        if stage >= 4:
            HX = 0
            HY = HX + 4096
            VB = HY + 8192
            H2B = VB + 4096
            VTT = H2B + 2048
            RR = VTT + 4096
            AT = RR + 2048
            T3 = AT + 1536
            H2T = T3 + 8192
            PBB = H2T + 2048
            PT = PBB + 1024
            GV = PT + 1024
            WPL = GV + 16384
            WG2 = WPL + 4096
            END3B = WG2 + 4096
            assert END3B <= ACTLIM, (END3B, ACTLIM)
            hx = view(HX, [128, D], F32)
            hy = [view(HY + 4096 * i, [128, D], F32) for i in range(2)]
            vbs = [view(VB + 2048 * i, [128, D], BF16) for i in range(2)]
            h2b = view(H2B, [128, D], BF16)
            vT = view(VTT, [128, 8, 256], BF16)
            rr = [view(RR + 1024 * i, [128, 256], F32) for i in range(2)]
            aT = [view(AT + 512 * i, [128, 256], BF16) for i in range(3)]
            t3 = [view(T3 + 4096 * i, [128, D], F32) for i in range(2)]
            h2Ts = [view(H2T, [128, 8, 128], BF16), view(GV + 12288 + 2048, [128, 8, 128], BF16)]
            pbb = view(PBB, [128, 2, 256], BF16)
            pT = view(PT, [128, 2, 256], BF16)
            g3bc = view(GV, [128, D], F32)
            g4bc = view(GV + 4096, [128, D], F32)
            g5bc = view(GV + 8192, [128, D], F32)
            bgrow = view(GV + 12288, [1, D], BF16)
            wple = view(WPL, [128, 2, D], BF16)
            wgB = view(WG2, [128, 2, D], BF16)

            def wg(c):
                return wgA[:, c, :] if c < 6 else wgB[:, c - 6, :]

            def wgk(c):
                return "wgA" if c < 6 else "wgB"

            add("pool", lambda e: e.dma_start(out=bgrow, in_=bg_d.ap()), writes=["bgrow"], dma=chan("bgr"))
            for nm, dst, src in (("g3bc", g3bc, g3_d), ("g4bc", g4bc, g4_d), ("g5bc", g5bc, g5_d)):
                add("sp", lambda e, dst=dst, src=src: e.dma_start(out=dst, in_=dram_ap(src, 0, [[0, 128], [1, D]])),
                    writes=[nm], dma=chan("c6" + nm))
            add("pool", lambda e: e.dma_start(out=wple, in_=dram_ap(wple_d, 0, [[D, 128], [128 * D, 2], [1, D]])),
                writes=["wple"], dma=chan("wpl"))
            add("pool", lambda e: e.dma_start(out=wgB, in_=dram_ap(wg_d, 6 * 128 * D, [[D, 128], [128 * D, 2], [1, D]])),
                writes=["wgB"], dma=chan("wgB"))

            NT2 = S // 256

            def xpre_elem(tt, sub):
                t = 2 * tt + sub
                add("sp", lambda e: e.dma_start(out=hx, in_=h1d.ap()[t * 128:(t + 1) * 128, :]),
                    reads=[("h1d", t)], writes=["hx"], dma=chan("hx"))
                ssq = stat[:, t:t + 1]
                rs = stat[:, 32 + t:33 + t]
                add("act", lambda e: e.activation(out=vbs[sub], in_=hx, func=AF.Square, accum_out=ssq),
                    reads=["hx"], writes=[("vb", sub), ("ssq", t)])
                rstd_ops(ssq, rs, D, ("ssq", t), ("rs", t))
                add("dve", lambda e: e.scalar_tensor_tensor(
                    out=vbs[sub], in0=hx, scalar=rs, in1=g3bc, op0=ALU.mult, op1=ALU.mult),
                    reads=["hx", ("rs", t), "g3bc"], writes=[("vb", sub)])

            def xpre_pe(tt, subs=(0, 1)):
                for sub in subs:
                    b = 7
                    for c in range(8):
                        add("pe", lambda e, c=c, b=b, sub=sub: e.transpose(
                            out=bank_bf(b)[:, c * 128:(c + 1) * 128], in_=vbs[sub][:, c * 128:(c + 1) * 128], identity=ident),
                            reads=[("vb", sub), "ident"], writes=[pk(b)])
                    src = bank_bf(b).rearrange("p (a b) -> p a b", a=8)
                    tok = slice(sub * 128, sub * 128 + 128)
                    if sub == 0:
                        add("act", lambda e, tok=tok, src=src: e.activation(out=vT[:, :, tok], in_=src, func=AF.Copy),
                            reads=[pk(b)], writes=[("vT", sub)])
                    else:
                        add("dve", lambda e, tok=tok, src=src: e.tensor_copy(out=vT[:, :, tok], in_=src),
                            reads=[pk(b)], writes=[("vT", sub)])

            def up(fc):
                ub_ = 4 + fc % 3
                for c in range(8):
                    add("pe", lambda e, c=c: e.matmul(
                        bank(ub_)[:, 0:256], lhsT=wup[:, c, fc * 128:(fc + 1) * 128], rhs=vT[:, c, :],
                        start=(c == 0), stop=(c == 7)),
                        reads=[("wup", c), ("vT", 0), ("vT", 1)], writes=[pk(ub_)])
                r_ = rr[fc % 2]
                a_ = aT[fc % 3]
                add("act", lambda e: e.activation(out=r_, in_=bank(ub_)[:, 0:256], func=AF.Relu),
                    reads=[pk(ub_)], writes=[("rr", fc % 2)])
                add("dve", lambda e: e.tensor_tensor(out=a_, in0=r_, in1=r_, op=ALU.mult),
                    reads=[("rr", fc % 2)], writes=[("aT", fc % 3)])

            def down(fc):
                a_ = aT[fc % 3]
                for s_ in range(2):
                    for n in range(2):
                        add("pe", lambda e, s_=s_, n=n: e.matmul(
                            bank(2 * s_ + n), lhsT=a_[:, s_ * 128:(s_ + 1) * 128], rhs=wdn[:, fc, n * 512:(n + 1) * 512],
                            start=(fc == 0), stop=(fc == 31)),
                            reads=[("aT", fc % 3), ("wdn", fc // 4)], writes=[pk(2 * s_ + n)])

            def y_head(tt):
                add("act", lambda e: e.activation(out=t3[0], in_=ps2[0][:], func=AF.Copy),
                    reads=[pk(0), pk(1)], writes=[("t3", 0)])
                add("dve", lambda e: e.tensor_copy(out=t3[1], in_=ps2[1][:]),
                    reads=[pk(2), pk(3)], writes=[("t3", 1)])

            def y_norm(tt, sub):
                t = 2 * tt + sub
                ssqf = stat[:, 64 + t:65 + t]
                rsf = stat[:, 96 + t:97 + t]
                add("act", lambda e: e.activation(out=h2b, in_=t3[sub], func=AF.Square, accum_out=ssqf),
                    reads=[("t3", sub)], writes=["h2b", ("ssqf", t)])
                rstd_ops(ssqf, rsf, D, ("ssqf", t), ("rsf", t))
                add("dve", lambda e: e.scalar_tensor_tensor(
                    out=t3[sub], in0=t3[sub], scalar=rsf, in1=g4bc, op0=ALU.mult, op1=ALU.mult),
                    reads=[("t3", sub), ("rsf", t), "g4bc"], writes=[("t3", sub)])

            def y_loads(tt):
                for sub in range(2):
                    t = 2 * tt + sub
                    add("sp", lambda e, t=t, sub=sub: e.dma_start(out=hy[sub], in_=h1d.ap()[t * 128:(t + 1) * 128, :]),
                        reads=[("h1d", t)], writes=[("hy", sub)], dma=chan(f"hy{sub}"))
                add("pool", lambda e: e.dma_start(
                    out=pbb, in_=dram_ap(p_d, tt * 256 * PLE, [[PLE, 128], [128 * PLE, 2], [1, PLE]])),
                    writes=["pbb"], dma=chan("pb"))

            def y_ptrans(tt):
                for sub in range(2):
                    for c2 in range(2):
                        add("pe", lambda e, sub=sub, c2=c2: e.transpose(
                            out=bank_bf(7)[:, (2 * sub + c2) * 128:(2 * sub + c2 + 1) * 128],
                            in_=pbb[:, sub, c2 * 128:(c2 + 1) * 128], identity=ident),
                            reads=["pbb", "ident"], writes=[pk(7)])
                for sub in range(2):
                    srcp = bank_bf(7)[:, sub * 256:(sub + 1) * 256].rearrange("p (c t) -> p c t", c=2)
                    add("dve", lambda e, sub=sub, srcp=srcp: e.tensor_copy(out=pT[:, :, sub * 128:(sub + 1) * 128], in_=srcp),
                        reads=[pk(7)], writes=[("pT", sub)])

            def y_s1(tt, sub):
                add("dve", lambda e: e.tensor_tensor(out=h2b, in0=t3[sub], in1=hy[sub], op=ALU.add),
                    reads=[("t3", sub), ("hy", sub)], writes=["h2b"])
                add("pool", lambda e: e.tensor_tensor(out=t3[sub], in0=t3[sub], in1=hy[sub], op=ALU.add),
                    reads=[("t3", sub), ("hy", sub)], writes=[("t3", sub)])

            def y_s2(tt, sub):
                for c in range(8):
                    add("pe", lambda e, c=c: e.transpose(out=bank_bf(7)[:, c * 128:(c + 1) * 128],
                                                         in_=h2b[:, c * 128:(c + 1) * 128], identity=ident),
                        reads=["h2b", "ident"], writes=[pk(7)])
                src = bank_bf(7).rearrange("p (a b) -> p a b", a=8)
                add("dve", lambda e: e.tensor_copy(out=h2Ts[sub], in_=src), reads=[pk(7)], writes=[("h2T", sub)])

            def y_half(tt, sub, n):
                tok = slice(sub * 128, sub * 128 + 128)
                cs = slice(n * 512, (n + 1) * 512)
                gpv = hy[sub][:, cs]
                k = ("hy", sub)
                for c in range(8):
                    add("pe", lambda e, c=c: e.matmul(
                        bank(7), lhsT=h2Ts[sub][:, c, :], rhs=wg(c)[:, cs], start=(c == 0), stop=False),
                        reads=[("h2T", sub), wgk(c)], writes=[pk(7)])
                add("pe", lambda e: e.matmul(bank(7), lhsT=onesrow, rhs=bgrow[:, cs], start=False, stop=True),
                    reads=["onesrow", "bgrow"], writes=[pk(7)])
                add("act", lambda e: e.activation(out=gpv, in_=bank(7), func=AF.Sigmoid), reads=[pk(7), k], writes=[k])
                for c2 in range(2):
                    add("pe", lambda e, c2=c2: e.matmul(
                        bank(7), lhsT=pT[:, c2, tok], rhs=wple[:, c2, cs], start=(c2 == 0), stop=(c2 == 1)),
                        reads=[("pT", sub), "wple"], writes=[pk(7)])
                add("dve", lambda e: e.tensor_tensor(out=gpv, in0=bank(7), in1=gpv, op=ALU.mult),
                    reads=[pk(7), k], writes=[k])

            def y_tail(tt, sub):
                t = 2 * tt + sub
                k = ("hy", sub)
                ssqg = stat[:, 128 + t:129 + t]
                rsg = stat[:, 160 + t:161 + t]
                add("act", lambda e: e.activation(out=h2b, in_=hy[sub], func=AF.Square, accum_out=ssqg),
                    reads=[k], writes=["h2b", ("ssqg", t)])
                rstd_ops(ssqg, rsg, D, ("ssqg", t), ("rsg", t))
                add("dve", lambda e: e.scalar_tensor_tensor(
                    out=hy[sub], in0=hy[sub], scalar=rsg, in1=g5bc, op0=ALU.mult, op1=ALU.mult),
                    reads=[k, ("rsg", t), "g5bc"], writes=[k])
                add("pool", lambda e: e.tensor_tensor(out=t3[sub], in0=t3[sub], in1=hy[sub], op=ALU.add),
                    reads=[("t3", sub), k], writes=[("t3", sub)])
                add("sp", lambda e: e.dma_start(out=y_d.ap()[t * 128:(t + 1) * 128, :], in_=t3[sub]),
                    reads=[("t3", sub)], writes=[("y", t)], dma=chan(f"ys{sub}"))

            def y_early(tt):
                y_loads(tt)
                y_norm(tt, 0)
                y_norm(tt, 1)
                y_ptrans(tt)

            def y_rest(tt):
                for sub in range(2):
                    y_s1(tt, sub)
                    y_s2(tt, sub)
                    y_half(tt, sub, 0)
                    y_half(tt, sub, 1)
                    y_tail(tt, sub)

            xpre_elem(0, 0)
            xpre_elem(0, 1)
            xpre_pe(0)
            for tt in range(NT2):
                ylist = captured(lambda: y_rest(tt - 1)) if tt >= 1 else []
                ylv = levelize(ylist)
                ymax = (max(ylv) + 1) if ylv else 0
                xlist = captured(lambda: (xpre_elem(tt + 1, 0), xpre_elem(tt + 1, 1))) if tt + 1 < NT2 else []
                xlv = levelize(xlist)
                xmax = (max(xlv) + 1) if xlv else 0
                if tt == 1 and debug:
                    print("Y levels:", ymax, "ops", len(ylist), "X levels:", xmax, len(xlist))
                up(0)
                up(1)
                for fc in range(32):
                    if fc + 2 < 32:
                        up(fc + 2)
                    if fc == 29 and tt + 1 < NT2:
                        xpre_pe(tt + 1, (0,))
                    if fc == 30 and tt + 1 < NT2:
                        xpre_pe(tt + 1, (1,))
                    down(fc)
                    if fc == 1 and tt >= 1:
                        y_early(tt - 1)
                    if 3 <= fc < 27:
                        if ylist:
                            replay_levels(ylist, ylv, ((fc - 2) * ymax) // 24 - 1)
                    if 12 <= fc < 26 and xlist:
                        replay_levels(xlist, xlv, ((fc - 11) * xmax) // 14 - 1)
                replay(ylist, len(ylist))
                replay(xlist, len(xlist))
                y_head(tt)
            y_early(NT2 - 1)
            y_rest(NT2 - 1)

        sch.barrier()

        sch.finalize()
        with nc.Block() as block:
            @block.tensor
            def _(e):
                sch.emit("pe", e, eng_sems, dma_sems)

            @block.scalar
            def _(e):
                sch.emit("act", e, eng_sems, dma_sems)

            @block.vector
            def _(e):
                sch.emit("dve", e, eng_sems, dma_sems)

            @block.gpsimd
            def _(e):
                sch.emit("pool", e, eng_sems, dma_sems)

            @block.sync
            def _(e):
                sch.emit("sp", e, eng_sems, dma_sems)
    return nc


_CONSTS = None


def _consts():
    global _CONSTS
    if _CONSTS is None:
        _CONSTS = dict(
            c_ident=np.eye(128, dtype=np.float32),
            c_jrev=np.ascontiguousarray(np.eye(128, dtype=np.float32)[::-1]),
            c_onehot=_onehot_tables(),
        )
    return _CONSTS


def make_in_maps(x, p, rel_bias_table, g_pre_mix, w_in, sink_a, g_out_a, g_out_b, w_o,
                 g_post_mix, g_pre_mlp, w_up, w_down, g_post_mlp, w_ple_proj, w_ple_gate,
                 b_ple_gate, g_post_ple):
    f = lambda a: np.ascontiguousarray(np.asarray(a, dtype=np.float32))
    shared = dict(
        rel_bias_table=f(rel_bias_table), g_pre_mix=f(g_pre_mix[0:1]), w_in=f(w_in[0]), sink_a=f(sink_a[0:1]),
        g_out=f(np.concatenate([np.asarray(g_out_a[0]), np.asarray(g_out_b[0])])[None, :]),
        w_o=f(w_o[0]), g_post_mix=f(g_post_mix[0:1]), g_pre_mlp=f(g_pre_mlp[0:1]), w_up=f(w_up[0]),
        w_down=f(w_down[0]), g_post_mlp=f(g_post_mlp[0:1]), w_ple_proj=f(w_ple_proj[0]),
        w_ple_gate=f(w_ple_gate[0]), b_ple_gate=f(b_ple_gate[0:1]), g_post_ple=f(g_post_ple[0:1]),
    )
    shared.update(_consts())
    x = np.asarray(x)
    p = np.asarray(p)
    maps = []
    for b in range(NCORES):
        m = dict(shared)
        m["x"] = f(x[b])
        m["p"] = f(p[0, b])
        maps.append(m)
    return maps


def kernel(**inputs):
    in_maps = make_in_maps(**inputs)
    nc = build_nc()
    res = run_bass_kernel_spmd(nc, in_maps, core_ids=list(range(NCORES)))
    out = np.stack([np.asarray(r["y"], dtype=np.float32) for r in res.results], axis=0)
    return out
```
